# Optimizing a Trainium2 kernel written in Bass

```python
import math
import jax
import jax.numpy as jnp
from jax import lax
import numpy as np

D_MODEL = 1024
BATCH = 32
SEQ = 256
DEPTH = 4
DEC_BATCH = 4
DEC_SEQ = 2048
PAST_LEN = 512

GRID_W = 64
MIX_W = D_MODEL
GROUP_W = MIX_W // 4
N_IN_BLOCKS = 10
D_FF = 4 * D_MODEL
EPS = 1e-6
F_MIN = 1e-30
ADA_CHUNKS = 6
CONV_K = 31
CONV_PAD = CONV_K // 2
GM_CHUNK = 128
GM_HEADS = 4
GM_HD = GROUP_W // GM_HEADS
HG_HEADS = 4
HG_DK = GROUP_W // HG_HEADS
HG_CHUNK = 64
S5_IN = 16
S5_GROUPS = GROUP_W // S5_IN
S5_P = 64
POS_BASE = 10000.0

kernel_name = 'hybrid_diffusion_ctx_prefix_step'


def rms_norm(x, g):
    xf = x.astype(jnp.float32)
    y = xf * lax.rsqrt(jnp.mean(jnp.square(xf), axis=-1, keepdims=True) + EPS)
    return (y * g.astype(jnp.float32)).astype(x.dtype)


def layer_norm(x, g, b):
    xf = x.astype(jnp.float32)
    mu = jnp.mean(xf, axis=-1, keepdims=True)
    var = jnp.mean(jnp.square(xf - mu), axis=-1, keepdims=True)
    y = (xf - mu) * lax.rsqrt(var + EPS)
    return (y * g.astype(jnp.float32) + b.astype(jnp.float32)).astype(x.dtype)


def grid_pos_embed(n_tokens, dtype):
    rows = n_tokens // GRID_W
    r, col = jnp.meshgrid(jnp.arange(rows, dtype=jnp.float32), jnp.arange(GRID_W, dtype=jnp.float32), indexing='ij')
    r = r.reshape(-1)
    col = col.reshape(-1)
    quarter = D_MODEL // 4
    freq = jnp.exp(-math.log(POS_BASE) * jnp.arange(quarter, dtype=jnp.float32) / quarter)
    ar = r[:, None] * freq
    ac = col[:, None] * freq
    return jnp.concatenate([jnp.sin(ar), jnp.cos(ar), jnp.sin(ac), jnp.cos(ac)], axis=-1).astype(dtype)


def conformer_conv(a_val, a_gate, w, b, ln_g, ln_b):
    u = a_val * jax.nn.sigmoid(a_gate)
    u = lax.conv_general_dilated(u, w[:, None, :].astype(u.dtype), window_strides=(1,),
                                 padding=[(CONV_PAD, CONV_PAD)], dimension_numbers=('NWC', 'WIO', 'NWC'),
                                 feature_group_count=GROUP_W)
    return jax.nn.silu(layer_norm(u + b, ln_g, ln_b))


def chunk_gmlp(u, v, g, ws, bs):
    bsz, t, _ = u.shape
    n = t // GM_CHUNK
    vh = rms_norm(v, g).reshape(bsz, n, GM_CHUNK, GM_HEADS, GM_HD)
    sv = jnp.einsum('hts,bnshd->bnthd', ws, vh) + jnp.transpose(bs)[None, None, :, :, None]
    return u * sv.reshape(bsz, t, GROUP_W)


def hgrn_scan(q, k, v, log_f, s0):
    bsz, t, h, _ = q.shape
    n = t // HG_CHUNK

    def chunks(a):
        return jnp.moveaxis(a.reshape(bsz, n, HG_CHUNK, h, a.shape[-1]), 1, 0)

    lower = jnp.tril(jnp.ones((HG_CHUNK, HG_CHUNK), dtype=bool))[None, :, :, None, None]

    def step(S, inp):
        qc, kc, vc, lf = inp
        b = jnp.cumsum(lf, axis=1)
        b_end = b[:, -1]
        inter = jnp.einsum('blhk,bhkv->blhv', qc * jnp.exp(b), S)
        diff = jnp.minimum(b[:, :, None] - b[:, None, :], 0.0)
        decay = jnp.where(lower, jnp.exp(diff), 0.0)
        scores = jnp.einsum('bthk,bshk,btshk->bhts', qc, kc, decay)
        intra = jnp.einsum('bhts,bshv->bthv', scores, vc)
        S = jnp.exp(b_end)[..., None] * S + jnp.einsum('bshk,bshv->bhkv', kc * jnp.exp(b_end[:, None] - b), vc)
        return S, inter + intra

    s_end, o = lax.scan(step, s0, (chunks(q), chunks(k), chunks(v), chunks(log_f)))
    return jnp.moveaxis(o, 0, 1).reshape(bsz, t, h, v.shape[-1]), s_end


def hgrn_mixer(zq, zi, zg, zf_fwd, zf_bwd, lb_fwd, lb_bwd, norm_g, s0):
    bsz, t, _ = zq.shape

    def heads(a):
        return a.astype(jnp.float32).reshape(bsz, t, HG_HEADS, HG_DK)

    q = jax.nn.silu(heads(zq))
    v = heads(zi)

    def forget(zf, lb):
        lb = lb.astype(jnp.float32).reshape(HG_HEADS, HG_DK)
        f = lb + (1.0 - lb) * jax.nn.sigmoid(heads(zf))
        return jnp.log(jnp.maximum(f, F_MIN)), 1.0 - f

    lf_f, k_f = forget(zf_fwd, lb_fwd)
    lf_b, k_b = forget(zf_bwd, lb_bwd)
    s0 = s0.astype(jnp.float32)
    o_f, s_f = hgrn_scan(q, k_f, v, lf_f, s0[:, 0])
    o_b, s_b = hgrn_scan(jnp.flip(q, 1), jnp.flip(k_b, 1), jnp.flip(v, 1), jnp.flip(lf_b, 1), s0[:, 1])
    o = o_f + jnp.flip(o_b, 1)
    o = rms_norm(o, norm_g.reshape(HG_HEADS, HG_DK)).reshape(bsz, t, GROUP_W)
    out = o * jax.nn.silu(zg.astype(jnp.float32))
    return out.astype(zq.dtype), jnp.stack([s_f, s_b], axis=1)


def s5_scan(u, a_re, a_im, log_dt, b_re, b_im, c_re, c_im, s0_re, s0_im):
    A = lax.complex(a_re, a_im)
    dt = jnp.exp(log_dt)[:, None]
    a_bar = jnp.exp(A * dt)
    b_bar = ((a_bar - 1.0) / A)[..., None] * lax.complex(b_re, b_im)
    bu = jnp.einsum('gpc,btgc->btgp', b_bar, u.astype(jnp.complex64))
    a_seq = jnp.broadcast_to(a_bar, bu.shape)

    def compose(earlier, later):
        return later[0] * earlier[0], later[0] * earlier[1] + later[1]

    a_cum, h = lax.associative_scan(compose, (a_seq, bu), axis=1)
    h = h + a_cum * lax.complex(s0_re, s0_im)[:, None]
    y = jnp.einsum('gcp,btgp->btgc', lax.complex(c_re, c_im), h).real
    return y, h[:, -1]


def s5_mixer(zu, a_re, a_im, log_dt, b_re, b_im, c_re, c_im, d, glu_w, glu_b, s0_re, s0_im):
    bsz, t, _ = zu.shape
    f32 = lambda a: a.astype(jnp.float32)
    u = f32(zu).reshape(bsz, t, S5_GROUPS, S5_IN)

    def run(i, uu):
        return s5_scan(uu, f32(a_re[i]), f32(a_im[i]), f32(log_dt[i]), f32(b_re[i]), f32(b_im[i]),
                       f32(c_re[i]), f32(c_im[i]), f32(s0_re[:, i]), f32(s0_im[:, i]))

    y_f, h_f = run(0, u)
    y_b, h_b = run(1, jnp.flip(u, 1))
    y = (y_f + jnp.flip(y_b, 1)).reshape(bsz, t, GROUP_W) + f32(d) * f32(zu)
    y = jax.nn.gelu(y)
    out = y * jax.nn.sigmoid(jnp.dot(y, f32(glu_w)) + f32(glu_b))
    h_end = jnp.stack([h_f, h_b], axis=1)
    return out.astype(zu.dtype), jnp.real(h_end), jnp.imag(h_end)


def token_mixing(h, hg_s0, s5_s0_re, s5_s0_im, P, l):
    z = jnp.dot(h, P['w_in'][l])
    a_val, a_gate, g_u, g_v, h_q, h_i, h_g, h_ff, h_fb, s5_u = jnp.split(z, N_IN_BLOCKS, axis=-1)
    o_conv = conformer_conv(a_val, a_gate, P['conv_w'][l], P['conv_b'][l], P['conv_ln_g'][l], P['conv_ln_b'][l])
    o_gm = chunk_gmlp(g_u, g_v, P['gmlp_norm_g'][l], P['gmlp_ws'][l], P['gmlp_bs'][l])
    o_hg, hg_end = hgrn_mixer(h_q, h_i, h_g, h_ff, h_fb, P['hgrn_lb'][0, l], P['hgrn_lb'][1, l],
                              P['hgrn_norm_g'][l], hg_s0)
    o_s5, s5_re, s5_im = s5_mixer(s5_u, P['s5_a_re'][l], P['s5_a_im'][l], P['s5_log_dt'][l], P['s5_b_re'][l],
                                  P['s5_b_im'][l], P['s5_c_re'][l], P['s5_c_im'][l], P['s5_d'][l],
                                  P['s5_glu_w'][l], P['s5_glu_b'][l], s5_s0_re, s5_s0_im)
    mixed = jnp.concatenate([o_conv, o_gm, o_hg, o_s5], axis=-1)
    return jnp.dot(mixed, P['w_out'][l]), hg_end, s5_re, s5_im


def trunk_layer(x, cond, hg_s0, s5_s0_re, s5_s0_im, P, l):
    ada = jnp.dot(jax.nn.silu(cond), P['ada_w'][l]) + P['ada_b'][l]
    sh1, sc1, g1, sh2, sc2, g2 = jnp.split(ada[:, None, :], ADA_CHUNKS, axis=-1)
    h = rms_norm(x, P['norm1_g'][l]) * (1.0 + sc1) + sh1
    mix, hg_end, s5_re, s5_im = token_mixing(h, hg_s0, s5_s0_re, s5_s0_im, P, l)
    x = x + g1 * mix
    h = rms_norm(x, P['norm2_g'][l]) * (1.0 + sc2) + sh2
    x = x + g2 * jnp.dot(jnp.square(jax.nn.relu(jnp.dot(h, P['mlp_w1'][l]))), P['mlp_w2'][l])
    return x, hg_end, s5_re, s5_im


def setup_inputs(seed: int = 0) -> dict:
    key = jax.random.key(seed)
    keys = list(jax.random.split(key, 48))

    def nrm(shape, scale):
        return scale * jax.random.normal(keys.pop(), shape, jnp.float32)

    def gain(shape):
        return 1.0 + nrm(shape, 0.02)

    L = DEPTH
    inputs = {}
    inputs['x_prompt'] = nrm((BATCH, SEQ, D_MODEL), 1.0)
    inputs['x_sample'] = nrm((DEC_BATCH, DEC_SEQ, D_MODEL), 1.0)
    inputs['state_hgrn'] = nrm((DEC_BATCH, L, 2, HG_HEADS, HG_DK, HG_DK), 0.5)
    inputs['state_s5_re'] = nrm((DEC_BATCH, L, 2, S5_GROUPS, S5_P), 0.3)
    inputs['state_s5_im'] = nrm((DEC_BATCH, L, 2, S5_GROUPS, S5_P), 0.3)
    inputs['c'] = nrm((DEC_BATCH, D_MODEL), 1.0)
    inputs['c_ctx'] = nrm((D_MODEL,), 1.0)
    inputs['norm1_g'] = gain((L, D_MODEL))
    inputs['norm2_g'] = gain((L, D_MODEL))
    inputs['ada_w'] = nrm((L, D_MODEL, ADA_CHUNKS * D_MODEL), 0.5 * D_MODEL ** -0.5)
    inputs['ada_b'] = nrm((L, ADA_CHUNKS * D_MODEL), 0.02)
    inputs['w_in'] = nrm((L, D_MODEL, N_IN_BLOCKS * GROUP_W), D_MODEL ** -0.5)
    inputs['conv_w'] = nrm((L, CONV_K, GROUP_W), CONV_K ** -0.5)
    inputs['conv_b'] = nrm((L, GROUP_W), 0.02)
    inputs['conv_ln_g'] = gain((L, GROUP_W))
    inputs['conv_ln_b'] = nrm((L, GROUP_W), 0.02)
    inputs['gmlp_norm_g'] = gain((L, GROUP_W))
    inputs['gmlp_ws'] = nrm((L, GM_HEADS, GM_CHUNK, GM_CHUNK), GM_CHUNK ** -0.5)
    inputs['gmlp_bs'] = nrm((L, GM_HEADS, GM_CHUNK), 0.02)
    inputs['hgrn_lb_logits'] = nrm((2, L, GROUP_W), 0.1)
    inputs['hgrn_norm_g'] = gain((L, GROUP_W))
    inputs['s5_a_re'] = -0.5 + nrm((L, 2, S5_GROUPS, S5_P), 0.01)
    inputs['s5_a_im'] = math.pi * jnp.arange(S5_P, dtype=jnp.float32) + nrm((L, 2, S5_GROUPS, S5_P), 0.01)
    inputs['s5_log_dt'] = jax.random.uniform(keys.pop(), (L, 2, S5_GROUPS), jnp.float32,
                                             math.log(1e-3), math.log(1e-1))
    inputs['s5_b_re'] = nrm((L, 2, S5_GROUPS, S5_P, S5_IN), (2 * S5_IN) ** -0.5)
    inputs['s5_b_im'] = nrm((L, 2, S5_GROUPS, S5_P, S5_IN), (2 * S5_IN) ** -0.5)
    inputs['s5_c_re'] = nrm((L, 2, S5_GROUPS, S5_IN, S5_P), (2 * S5_P) ** -0.5)
    inputs['s5_c_im'] = nrm((L, 2, S5_GROUPS, S5_IN, S5_P), (2 * S5_P) ** -0.5)
    inputs['s5_d'] = nrm((L, GROUP_W), 0.5)
    inputs['s5_glu_w'] = nrm((L, GROUP_W, GROUP_W), GROUP_W ** -0.5)
    inputs['s5_glu_b'] = nrm((L, GROUP_W), 0.02)
    inputs['w_out'] = nrm((L, MIX_W, D_MODEL), MIX_W ** -0.5)
    inputs['mlp_w1'] = nrm((L, D_MODEL, D_FF), D_MODEL ** -0.5)
    inputs['mlp_w2'] = nrm((L, D_FF, D_MODEL), D_FF ** -0.5)
    inputs['final_norm_g'] = gain((D_MODEL,))
    return inputs


def reference(x_prompt, x_sample, state_hgrn, state_s5_re, state_s5_im, c, c_ctx,
              norm1_g, norm2_g, ada_w, ada_b, w_in, conv_w, conv_b, conv_ln_g, conv_ln_b,
              gmlp_norm_g, gmlp_ws, gmlp_bs, hgrn_lb_logits, hgrn_norm_g,
              s5_a_re, s5_a_im, s5_log_dt, s5_b_re, s5_b_im, s5_c_re, s5_c_im, s5_d, s5_glu_w, s5_glu_b,
              w_out, mlp_w1, mlp_w2, final_norm_g):
    lb_soft = jax.nn.softmax(hgrn_lb_logits.astype(jnp.float32), axis=1)
    hgrn_lb = jnp.cumsum(lb_soft, axis=1) - lb_soft[:, :1]
    P = {'norm1_g': norm1_g, 'norm2_g': norm2_g, 'ada_w': ada_w, 'ada_b': ada_b, 'w_in': w_in,
         'conv_w': conv_w, 'conv_b': conv_b, 'conv_ln_g': conv_ln_g, 'conv_ln_b': conv_ln_b,
         'gmlp_norm_g': gmlp_norm_g, 'gmlp_ws': gmlp_ws, 'gmlp_bs': gmlp_bs,
         'hgrn_lb': hgrn_lb, 'hgrn_norm_g': hgrn_norm_g,
         's5_a_re': s5_a_re, 's5_a_im': s5_a_im, 's5_log_dt': s5_log_dt, 's5_b_re': s5_b_re,
         's5_b_im': s5_b_im, 's5_c_re': s5_c_re, 's5_c_im': s5_c_im, 's5_d': s5_d,
         's5_glu_w': s5_glu_w, 's5_glu_b': s5_glu_b, 'w_out': w_out, 'mlp_w1': mlp_w1, 'mlp_w2': mlp_w2}

    xp = x_prompt
    bp = xp.shape[0]
    hg_zero = jnp.zeros((bp, 2, HG_HEADS, HG_DK, HG_DK), jnp.float32)
    s5_zero = jnp.zeros((bp, 2, S5_GROUPS, S5_P), jnp.float32)
    cond_ctx = c_ctx[None, :]
    hg_states, s5_re_states, s5_im_states = [], [], []
    for l in range(DEPTH):
        xp, hg_end, s5_re, s5_im = trunk_layer(xp, cond_ctx, hg_zero, s5_zero, s5_zero, P, l)
        hg_states.append(hg_end)
        s5_re_states.append(s5_re)
        s5_im_states.append(s5_im)
    y_prompt = rms_norm(xp, final_norm_g)
    new_state_hgrn = jnp.stack(hg_states, axis=1).astype(x_prompt.dtype)
    new_state_s5_re = jnp.stack(s5_re_states, axis=1).astype(x_prompt.dtype)
    new_state_s5_im = jnp.stack(s5_im_states, axis=1).astype(x_prompt.dtype)

    xs = x_sample + grid_pos_embed(x_sample.shape[1], x_sample.dtype)[None]
    for l in range(DEPTH):
        xs, _, _, _ = trunk_layer(xs, c, state_hgrn[:, l], state_s5_re[:, l], state_s5_im[:, l], P, l)
    y_sample = rms_norm(xs, final_norm_g)
    return (y_prompt, y_sample, new_state_hgrn, new_state_s5_re, new_state_s5_im)
```

```python
import math
import numpy as np
import concourse.bass as bass
import concourse.mybir as mybir
from concourse.bass_utils import run_bass_kernel_spmd
from contextlib import ExitStack

F32 = mybir.dt.float32
BF16 = mybir.dt.bfloat16
AF = mybir.ActivationFunctionType
ALU = mybir.AluOpType
AX = mybir.AxisListType

NDS = 40
L = 4
D = 1024
T = 2048
NT = 4
TT = 512
EPS = 1e-6
MIX = {"conv": True, "gmlp": True, "hgrn": True, "s5": True}
NLAYERS = L


class Buf:
    __slots__ = ("name", "w", "r")

    def __init__(self, name):
        self.name = name
        self.w = None
        self.r = []


class Op:
    __slots__ = ("eng", "idx", "fn", "deps", "is_dma", "sig", "sem", "val")

    def __init__(self, eng, idx, fn, is_dma):
        self.eng = eng
        self.idx = idx
        self.fn = fn
        self.is_dma = is_dma
        self.deps = []
        self.sig = False
        self.sem = None
        self.val = 0


class Prog:
    ENGS = ["pe", "act", "dve", "pool", "sp"]

    def __init__(self, nc, stack):
        self.nc = nc
        self.ops = {e: [] for e in self.ENGS}
        self.ndma = 0
        self.dma_last = [None] * NDS
        self.dma_cnt = [0] * NDS
        self.esem = {e: stack.enter_context(nc.semaphore("es_" + e)) for e in self.ENGS}
        self.dsem = [stack.enter_context(nc.semaphore("ds%d" % i)) for i in range(NDS)]
        self.bufs = {}

    def _tb(self, x):
        if isinstance(x, Buf):
            return x
        b = self.bufs.get(x)
        if b is None:
            b = Buf(x)
            self.bufs[x] = b
        return b

    def op(self, eng, fn, reads=(), writes=(), dma=False):
        o = Op(eng, len(self.ops[eng]), fn, dma)
        deps = {}
        reads = [self._tb(b) for b in reads]
        writes = [self._tb(b) for b in writes]

        def add(p, raw):
            if p is None:
                return
            if p.eng == eng and (not p.is_dma) and (not dma):
                if eng == "pe" or not raw:
                    return
            deps[id(p)] = p

        for b in reads:
            add(b.w, True)
        for b in writes:
            add(b.w, False)
            for r in b.r:
                add(r, False)
        if dma:
            k = self.ndma % NDS
            self.ndma += 1
            prev = self.dma_last[k]
            if prev is not None:
                deps[id(prev)] = prev
            self.dma_cnt[k] += 1
            o.sem = self.dsem[k]
            o.val = 16 * self.dma_cnt[k]
            o.sig = True
            self.dma_last[k] = o
        o.deps = list(deps.values())
        for p in o.deps:
            p.sig = True
        for b in reads:
            b.r.append(o)
        for b in writes:
            b.w = o
            b.r = []
        self.ops[eng].append(o)
        return o

    def barrier(self):
        lasts = [self.ops[e][-1] for e in self.ENGS if self.ops[e]] + [d for d in self.dma_last if d is not None]
        for e in self.ENGS:
            o = self.op(e, lambda h: h.nop())
            o.deps = [q for q in lasts if q.is_dma or q.eng != e]
            for q in o.deps:
                q.sig = True

    def emit(self):
        nc = self.nc
        for e in self.ENGS:
            c = 0
            for o in self.ops[e]:
                if o.is_dma:
                    continue
                if o.sig:
                    c += 1
                    o.sem = self.esem[e]
                    o.val = c
        ops = self.ops

        def run(e, h):
            known = {}
            for o in ops[e]:
                for p in o.deps:
                    key = id(p.sem)
                    if known.get(key, 0) >= p.val:
                        continue
                    h.wait_ge(p.sem, p.val)
                    known[key] = p.val
                ins = o.fn(h)
                if o.sig:
                    ins.then_inc(o.sem, 16 if o.is_dma else 1)

        with nc.Block() as block:
            @block.tensor
            def _(h):
                run("pe", h)

            @block.scalar
            def _(h):
                run("act", h)

            @block.vector
            def _(h):
                run("dve", h)

            @block.gpsimd
            def _(h):
                run("pool", h)

            @block.sync
            def _(h):
                run("sp", h)


def build_program(nlayers=NLAYERS, mix=MIX):
    nc = bass.Bass("TRN2", target_bir_lowering=False)
    st = ExitStack()
    p = Prog(nc, st)

    def din(name, shape):
        return nc.dram_tensor(name, list(shape), F32, kind="ExternalInput").ap()

    def dout(name, shape):
        return nc.dram_tensor(name, list(shape), F32, kind="ExternalOutput").ap()

    xin = din("xin", [D, T])
    cond = din("cond", [128, 8])
    mcar_d = din("mcar", [128, 1])
    norm1_g = din("norm1_gT", [128, L * 8])
    norm2_g = din("norm2_gT", [128, L * 8])
    ada_w = din("ada_w", [L, D, 6 * D])
    ada_b = din("ada_bT", [128, L * 48])
    w_in = din("w_in", [L, D, 2560])
    w_out = din("w_out", [L, D, D])
    mlp_w1 = din("mlp_w1", [L, D, 4 * D])
    mlp_w2 = din("mlp_w2", [L, 4 * D, D])
    final_g = din("final_norm_gT", [128, 8])
    gm_ng = din("gmlp_norm_g", [L, 256])
    gm_wsT = din("gmlp_wsT", [L, 4, 128, 128])
    gm_bs = din("gmlp_bs", [L, 4, 128])
    cv_w = din("conv_wT", [128, L * 2 * 31])
    cv_vec = din("conv_vecT", [128, L * 3 * 2])
    ident_d = din("ident", [128, 128])
    s5p_d = din("s5pT", [128, L * 48])
    s5BT_d = din("s5_BTblk", [L, 128, 2048])
    s5CT_d = din("s5_CTblk", [L, 128, 2048])
    s5vec_d = din("s5vecT", [128, L * 4])
    s5glu_d = din("s5_glu_w", [L, 256, 256])
    s5init_d = din("s5init", [128, L * 32])
    s5st_d = dout("s5st", [128, L * 256])
    hglb_d = din("hglbT", [128, 16])
    hgng_d = din("hgngT", [128, L * 2])
    hginit_d = din("hginit", [128, L * 256])
    hgmask_d = din("hgmask", [128, 128])
    hgcm_d = din("hgcmask", [128, 512])
    hgblk_d = din("hgblk", [128, 128])
    hgst_d = dout("hgst", [128, L * 2048])
    oscr = nc.dram_tensor("oscr", [256, T], F32).ap()
    ztok = [nc.dram_tensor("ztok%d" % i, [T, 256], F32).ap() for i in range(2)]
    yout = dout("yout", [D, T])
    zscr = nc.dram_tensor("zscr", [2560, T], F32).ap()

    def sb(name, shape, dt=F32):
        return st.enter_context(nc.sbuf_tensor("sb_" + name, list(shape), dt))

    x = sb("x", [128, 8, T])
    hraw = sb("hraw", [128, 8192])
    hb = hraw.bitcast(BF16)[:, :].rearrange("p (k t) -> p k t", k=8)
    hrawb = hraw.bitcast(BF16)
    mixed = sb("mixed", [128, 8, T], BF16)
    wbuf = [sb("wbuf%d" % i, [128, 4096], BF16) for i in range(2)]
    stg = [sb("stg%d" % i, [128, T]) for i in range(2)]
    tmp = [sb("tmp%d" % i, [128, TT]) for i in range(4)]
    sqb = [sb("sqb%d" % i, [128, TT], BF16) for i in range(3)]
    ones_bf = sb("ones_bf", [128, 128], BF16)
    consts = sb("consts", [128, 64])
    adasb = sb("adasb", [128, L * 48])
    adab = sb("adab", [128, L * 48])
    n1g = sb("n1g", [128, L * 8])
    n2g = sb("n2g", [128, L * 8])
    fng = sb("fng", [128, 8])
    gm1 = sb("gm1", [128, L * 8])
    gm2 = sb("gm2", [128, L * 8])
    csb = sb("csb", [128, 8])
    csbf = sb("csbf", [128, 8], BF16)
    zero8 = sb("zero8", [128, 8])
    mt = [hraw[:, i * 512:(i + 1) * 512] for i in range(12)]
    mb = [hrawb[:, 12288 + i * 512:12288 + (i + 1) * 512] for i in range(6)]
    ucp = sb("ucp", [128, 2, 8, 286], BF16)
    ident = sb("ident_sb", [128, 128])
    gB = sb("gB", [128, 256])
    wsT = sb("wsT", [128, 4, 128], BF16)
    bsB = sb("bsB", [128, 2, 128])
    cwt = sb("cwt", [128, L * 62])
    cvec = sb("cvec", [128, L * 6])
    dg = sb("dg", [128, 31, 128], BF16)
    sm = sb("sm", [128, 64])
    s5p = sb("s5p", [128, L * 48])
    s5vec = sb("s5vec", [128, L * 4])
    s5init = sb("s5init", [128, L * 32])
    s5o = sb("s5o", [128, 256])
    spw = sb("spw", [128, 20, 16])
    spi = sb("spi", [128, 16], mybir.dt.int32)
    s5c = sb("s5c", [128, 16])
    gluW = sb("gluW", [128, 2, 256], BF16)
    hgl = sb("hgl", [128, 64])
    hgng = sb("hgng", [128, L * 2])
    hgmask = sb("hgmask", [128, 128])
    hgcm = sb("hgcm", [128, 512])
    hgblk = sb("hgblk", [128, 128], BF16)
    identb = sb("identb", [128, 128], BF16)
    hgs = sb("hgs", [128, 48])
    hgS = sb("hgS", [128, 64])
    hgSo = [sb("hgSo%d" % i, [128, 64]) for i in range(2)]
    hgSb = sb("hgSb", [128, 128], BF16)
    pbank = [st.enter_context(nc.psum_tensor("pb%d" % i, [128, 512], F32)) for i in range(8)]

    def dma(q, out, in_, reads, writes):
        return p.op(q, lambda h: h.dma_start(out=out, in_=in_), reads, writes, dma=True)

    def act(out, in_, func, reads, writes, bias=0.0, scale=1.0):
        return p.op("act", lambda h: h.activation(out=out, in_=in_, func=func, bias=bias, scale=scale), reads, writes)

    def tt(eng, out, a, b, op, reads, writes):
        return p.op(eng, lambda h: h.tensor_tensor(out=out, in0=a, in1=b, op=op), reads, writes)

    def ts(eng, out, a, s1, s2, op0, op1, reads, writes):
        return p.op(eng, lambda h: h.tensor_scalar(out=out, in0=a, scalar1=s1, scalar2=s2, op0=op0, op1=op1), reads, writes)

    def stt(eng, out, a, s, b, op0, op1, reads, writes):
        return p.op(eng, lambda h: h.scalar_tensor_tensor(out=out, in0=a, scalar=s, in1=b, op0=op0, op1=op1), reads, writes)

    def mm(out, lhsT, rhs, start, stop, reads, writes, sgc=False):
        return p.op("pe", lambda h: h.matmul(out, lhsT, rhs, start=start, stop=stop, skip_group_check=sgc), reads, writes)

    def rev(ap, n):
        return bass.AP(ap.tensor, ap.offset + (n - 1), [[ap.ap[0][0], 128], [-1, n]])

    def cols(ap, c0, step, n):
        return bass.AP(ap.tensor, ap.offset + c0, [[ap.ap[0][0], 128], [step, n]])

    def copy(eng, out, in_, reads, writes):
        return p.op(eng, lambda h: h.tensor_copy(out=out, in_=in_), reads, writes)

    def memset(eng, ap, val, writes):
        return p.op(eng, lambda h: h.memset(ap, val), (), writes)

    cnt = {"ps": 0, "tmp": 0, "sq": 0, "w": 0, "stg": 0, "q": 0, "mt": 0, "mb": 0}

    def next_mt():
        i = cnt["mt"] % 12
        cnt["mt"] += 1
        return mt[i], "mt%d" % i

    def next_mb():
        i = cnt["mb"] % 6
        cnt["mb"] += 1
        return mb[i], "mb%d" % i

    def bcast_rows(ap, off, nrows, ncols):
        return bass.AP(ap.tensor, ap.offset + off, [[0, nrows], [1, ncols]])

    def next_ps():
        i = cnt["ps"] % 4
        cnt["ps"] += 1
        return pbank[i], "pb%d" % i

    def next_tmp():
        i = cnt["tmp"] % 4
        cnt["tmp"] += 1
        return tmp[i], "tmp%d" % i

    def next_sq():
        i = cnt["sq"] % 3
        cnt["sq"] += 1
        return sqb[i], "sqb%d" % i

    def next_w():
        i = cnt["w"] % 2
        cnt["w"] += 1
        return wbuf[i], "wbuf%d" % i

    def next_stg():
        i = cnt["stg"] % 2
        cnt["stg"] += 1
        return stg[i], "stg%d" % i

    def next_q():
        return "sp"

    def xn(k, t):
        return "x_%d_%d" % (k, t)

    def hn(k, t):
        return "h_%d_%d" % (k, t)

    def mn(k, t):
        return "m_%d_%d" % (k, t)

    tsl = lambda t: slice(t * TT, (t + 1) * TT)

    memset("dve", ones_bf[:], 1.0, ["ones"])
    memset("dve", zero8[:], 0.0, ["zero8"])
    dma("sp", adab[:], ada_b, [], ["adab"])
    dma("sp", n1g[:], norm1_g, [], ["n1g"])
    dma("sp", n2g[:], norm2_g, [], ["n2g"])
    dma("sp", fng[:], final_g, [], ["fng"])
    dma("sp", csb[:], cond, [], ["csb"])
    dma("sp", consts[:, 0:1], mcar_d, [], ["consts"])
    mcar = consts[:, 0:1]
    dma("sp", ident[:], ident_d, [], ["ident"])
    dma("sp", cwt[:], cv_w, [], ["cwt"])
    dma("sp", cvec[:], cv_vec, [], ["cvec"])
    memset("pool", ucp[:], 0.0, ["ucp"])
    dma("sp", s5p[:], s5p_d, [], ["s5p"])
    dma("sp", hgl[:, 0:16], hglb_d, [], ["hgl"])
    memset("pool", hgSb[:], 0.0, ["hgSb"])
    dma("sp", hgng[:], hgng_d, [], ["hgng"])
    dma("sp", hgmask[:], hgmask_d, [], ["hgmask"])
    dma("sp", hgcm[:], hgcm_d, [], ["hgcm"])
    dma("pool", hgblk[:], hgblk_d, [], ["hgblk"])
    dma("pool", identb[:], ident_d, [], ["identb"])
    act(hgl[:, 16:32], hgl[:, 0:16], AF.Exp, ["hgl"], ["hgl"])
    p.op("dve", lambda h: h.tensor_reduce(out=hgs[:, 40:44], in_=hgl[:, 16:32].rearrange("p (a l) -> p a l", l=4), axis=AX.X, op=ALU.add),
         ["hgl"], ["hgs"])
    p.op("dve", lambda h: h.reciprocal(out=hgs[:, 40:44], in_=hgs[:, 40:44]), ["hgs"], ["hgs"])
    tt("dve", hgl[:, 16:32].rearrange("p (a l) -> p a l", l=4), hgl[:, 16:32].rearrange("p (a l) -> p a l", l=4),
       bass.AP(hgs, 40, [[48, 128], [1, 4], [0, 4]]), ALU.mult, ["hgl", "hgs"], ["hgl"])
    lbv = hgl[:, 32:48].rearrange("p (a l) -> p a l", l=4)
    smv = hgl[:, 16:32].rearrange("p (a l) -> p a l", l=4)
    memset("dve", hgl[:, 32:48], 0.0, ["hgl"])
    for li in range(1, 4):
        tt("dve", lbv[:, :, li:li + 1], lbv[:, :, li - 1:li], smv[:, :, li:li + 1], ALU.add, ["hgl"], ["hgl"])
    ts("dve", hgl[:, 48:64], hgl[:, 32:48], -1.0, 1.0, ALU.mult, ALU.add, ["hgl"], ["hgl"])
    dma("sp", s5vec[:], s5vec_d, [], ["s5vec"])
    dma("sp", s5init[:], s5init_d, [], ["s5init"])

    for k in range(8):
        dma(next_q(), x[:, k, :], xin[k * 128:(k + 1) * 128, :], [], [xn(k, t) for t in range(NT)])
    act(csbf[:], csb[:], AF.Silu, ["csb"], ["csbf"])
    def ada_block(l, cb):
        w_, wn = next_w()
        w_ = w_[:, :].rearrange("p (k n) -> p k n", k=8)
        dma("pool", w_, ada_w[l][:, cb * 512:(cb + 1) * 512].rearrange("(k p) n -> p k n", p=128), [], [wn])
        for j in range(4):
            col = l * 48 + cb * 4 + j
            for k in range(8):
                mm(pbank[7][:, col:col + 1], w_[:, k, j * 128:(j + 1) * 128], csbf[:, k:k + 1], k == 0, k == 7,
                   [wn, "csbf"], ["pb7_%d" % l])

    abuf = sb("abuf", [128, 8, 64])
    csb32 = sb("csb32", [128, 8])
    act(csb32[:], csb[:], AF.Silu, ["csb"], ["csb32"])

    def ada_small(l, jj):
        dma("sp", abuf[:], ada_w[l][:, jj * 64:(jj + 1) * 64].rearrange("(k p) n -> p k n", p=128), [], ["abuf"])
        col = l * 48 + jj // 2
        ps_ = slice(64 * (jj % 2), 64 * (jj % 2) + 64)
        for k in range(8):
            mm(pbank[7][ps_, col:col + 1], abuf[:, k, :], csb32[:, k:k + 1], k == 0, k == 7, ["abuf", "csb32"], ["pb7_%d" % l])

    def ada_finish(l):
        b0 = l * 48
        tt("dve", adasb[:, b0:b0 + 48], pbank[7][:, b0:b0 + 48], adab[:, b0:b0 + 48], ALU.add, ["pb7_%d" % l, "adab"], ["ada%d" % l])
        stt("dve", gm1[:, l * 8:(l + 1) * 8], adasb[:, b0 + 8:b0 + 16], 1.0, n1g[:, l * 8:(l + 1) * 8], ALU.add, ALU.mult,
            ["ada%d" % l, "n1g"], ["gm1_%d" % l])
        stt("dve", gm2[:, l * 8:(l + 1) * 8], adasb[:, b0 + 32:b0 + 40], 1.0, n2g[:, l * 8:(l + 1) * 8], ALU.add, ALU.mult,
            ["ada%d" % l, "n2g"], ["gm2_%d" % l])

    I32 = mybir.dt.int32
    rowf, colf = stg[0], stg[1]
    p.op("pool", lambda h: h.iota(rowf[:, :], pattern=[[1, 32], [0, 64]], base=0, channel_multiplier=0, allow_small_or_imprecise_dtypes=True),
         (), ["stg0"])
    p.op("pool", lambda h: h.iota(colf[:, :], pattern=[[0, 32], [1, 64]], base=0, channel_multiplier=0, allow_small_or_imprecise_dtypes=True),
         (), ["stg1"])
    p.op("pool", lambda h: h.iota(consts[:, 8:10], pattern=[[128, 2]], base=0, channel_multiplier=1, allow_small_or_imprecise_dtypes=True),
         (), ["cfq"])
    for l_ in range(nlayers):
        for cb in range(12):
            ada_block(l_, cb)
    act(consts[:, 10:12], consts[:, 8:10], AF.Exp, ["cfq"], ["cfq2"], scale=-math.log(10000.0) / 256.0)
    ts("dve", consts[:, 12:14], consts[:, 10:12], 1.0 / (2.0 * math.pi), None, ALU.mult, ALU.bypass, ["cfq2"], ["cfq3"])
    for k in range(8):
        pv, pvn = (rowf, "stg0") if k < 4 else (colf, "stg1")
        kk = k % 2
        ph = 0.25 if (k // 2) % 2 == 1 else 0.0
        for t in range(NT):
            Y, Yn = next_mt()
            K, Kn = next_mt()
            M, Mn = next_mt()
            act(Y, pv[:, tsl(t)], AF.Identity, [pvn, "cfq3"], [Yn], scale=consts[:, 12 + kk:13 + kk], bias=ph)
            copy("dve", K.bitcast(I32), Y, [Yn], [Kn])
            copy("dve", M, K.bitcast(I32), [Kn], [Mn])
            tt("dve", Y, Y, M, ALU.subtract, [Yn, Mn], [Yn])
            ts("dve", M, Y, 0.5, None, ALU.is_gt, ALU.bypass, [Yn], [Mn])
            tt("dve", Y, Y, M, ALU.subtract, [Yn, Mn], [Yn])
            ts("dve", M, Y, -0.5, None, ALU.is_lt, ALU.bypass, [Yn], [Mn])
            tt("dve", Y, Y, M, ALU.add, [Yn, Mn], [Yn])
            act(Y, Y, AF.Sin, [Yn], [Yn], scale=6.283185)
            stt("dve", x[:, k, tsl(t)], Y, mcar, x[:, k, tsl(t)], ALU.mult, ALU.add, [Yn, "consts", xn(k, t)], [xn(k, t)])
    p.barrier()

    for l_ in range(nlayers):
        ada_finish(l_)

    cur_l = [0]

    def norm_mod(gm_ap, sh_ap, gname, out_fn):
        for t in range(NT):
            ps, psn = pbank[4 + (t % 2)], "pb%d" % (4 + (t % 2))
            for k in range(8):
                sq, sqn = next_sq()
                act(sq[:], x[:, k, tsl(t)], AF.Square, [xn(k, t)], [sqn])
                mm(ps[:], ones_bf[:], sq[:], k == 0, k == 7, [sqn, "ones"], [psn])
            sd, sdn = next_tmp()
            act(sd[:], ps[:], AF.Sqrt, [psn], [sdn], bias=EPS, scale=1.0 / D)
            rs, rsn = next_tmp()
            p.op("dve", lambda h, rs=rs, sd=sd: h.reciprocal(out=rs[:], in_=sd[:]), [sdn], [rsn])
            for k in range(8):
                tm, tmn = next_sq() if False else next_tmp()
                if tmn == rsn:
                    tm, tmn = next_tmp()
                tt("dve", tm[:], x[:, k, tsl(t)], rs[:], ALU.mult, [xn(k, t), rsn], [tmn])
                o_ap, o_n = out_fn(k, t)
                act(o_ap, tm[:], AF.Identity, [tmn, gname, "ada%d" % cur_l[0]], [o_n], bias=sh_ap(k), scale=gm_ap(k))


    def mixer_gmlp(l):
        dma("sp", gB[:], bcast_rows(gm_ng, l * 256, 128, 256), [], ["gB"])
        dma("pool", wsT[:], gm_wsT[l].rearrange("h s t -> s h t"), [], ["wsT"])
        for h in range(4):
            dma("sp", bsB[(h % 2) * 64:(h % 2) * 64 + 64, h // 2, :], bcast_rows(gm_bs, (l * 4 + h) * 128, 64, 128), [], ["bsB"])
        for t in range(NT):
            s_, sn = next_stg()
            vt = s_[:, 0:1024].rearrange("p (j n) -> p j n", n=256)
            dma(next_q(), vt, ztok[0][t * 512:(t + 1) * 512, :].rearrange("(j p) n -> p j n", p=128), ["ztok0"], [sn])
            junk, jn = next_mb()
            for j in range(4):
                p.op("act", lambda h, j=j, junk=junk, vt=vt, t=t: h.activation(out=junk[:, 0:256], in_=vt[:, j, :], func=AF.Square,
                                                                        accum_out=sm[:, t * 4 + j:t * 4 + j + 1]), [sn], [jn, "sm_g%d" % t])
            act(sm[:, 16 + t * 4:16 + t * 4 + 4], sm[:, t * 4:t * 4 + 4], AF.Sqrt, ["sm_g%d" % t], ["sm_h%d" % t], bias=EPS, scale=1.0 / 256)
            p.op("dve", lambda h, t=t: h.reciprocal(out=sm[:, 32 + t * 4:32 + t * 4 + 4], in_=sm[:, 16 + t * 4:16 + t * 4 + 4]),
                 ["sm_h%d" % t], ["sm_r%d" % t])
            vh, vhn = next_mb()
            vh2, vh2n = next_mb()
            vhs = [vh, vh2]
            for j in range(4):
                stt("dve", vhs[j // 2][:, (j % 2) * 256:(j % 2) * 256 + 256], vt[:, j, :], sm[:, 32 + t * 4 + j:32 + t * 4 + j + 1], gB[:],
                    ALU.mult, ALU.mult, [sn, "sm_r%d" % t, "gB"], [vhn if j < 2 else vh2n])
            for c in range(2):
                ps, psn = next_ps()
                for j in range(4):
                    for h2 in range(2):
                        hh = 2 * c + h2
                        src = vhs[j // 2][:, (j % 2) * 256 + hh * 64:(j % 2) * 256 + hh * 64 + 64]
                        mm(ps[h2 * 64:(h2 + 1) * 64, j * 128:(j + 1) * 128], src, wsT[:, hh, :], True, True,
                           [vhn, vh2n, "wsT"], [psn])
                svb, svn = next_mt()
                bb = bass.AP(bsB, c * 128, [[256, 128], [0, 4], [1, 128]])
                tt("dve", svb[:].rearrange("p (j t) -> p j t", j=4), ps[:].rearrange("p (j t) -> p j t", j=4), bb, ALU.add,
                   [psn, "bsB"], [svn])
                ut, utn = next_mt()
                dma(next_q(), ut[:], zscr[512 + c * 128:512 + (c + 1) * 128, tsl(t)], ["zscr2_%d" % c], [utn])
                tt("pool", mixed[:, 2 + c, tsl(t)], ut[:], svb[:], ALU.mult, [utn, svn], [mn(2 + c, t)])

    def mixer_conv(l):
        memset("pool", ucp[:], 0.0, ["ucp"])
        cb = lambda c: cvec[:, l * 6 + c:l * 6 + c + 1]
        lg = lambda c: cvec[:, l * 6 + 2 + c:l * 6 + 3 + c]
        lb = lambda c: cvec[:, l * 6 + 4 + c:l * 6 + 5 + c]
        for c in range(2):
            for t in range(NT):
                va, van = next_mt()
                ga, gan = next_mt()
                dma(next_q(), va[:], zscr[0 + c * 128:0 + (c + 1) * 128, tsl(t)], ["zscr0_%d" % c], [van])
                dma(next_q(), ga[:], zscr[256 + c * 128:256 + (c + 1) * 128, tsl(t)], ["zscr1_%d" % c], [gan])
                act(ga[:], ga[:], AF.Sigmoid, [gan], [gan])
                tt("dve", ucp[:, c, 2 * t:2 * t + 2, 15:271], va[:].rearrange("p (s n) -> p s n", s=2),
                   ga[:].rearrange("p (s n) -> p s n", s=2), ALU.mult, [van, gan], ["ucp"])
            ts("dve", ucp[:, c, 1:8, 0:15], ucp[:, c, 0:7, 256:271], mcar, None, ALU.mult, ALU.bypass, ["ucp", "consts"], ["ucp"])
            ts("dve", ucp[:, c, 0:7, 271:286], ucp[:, c, 1:8, 15:30], mcar, None, ALU.mult, ALU.bypass, ["ucp", "consts"], ["ucp"])
        for c in range(2):
            for k in range(31):
                ts("pool", dg[:, k, :], ident[:], cwt[:, l * 62 + c * 31 + k:l * 62 + c * 31 + k + 1], 1.0, ALU.mult, ALU.mult,
                   ["ident", "cwt"], ["dg"])
            for t in range(NT):
                ps, psn = next_ps()
                for k in range(31):
                    mm(ps[:].rearrange("p (s n) -> p s n", s=2), dg[:, k, :], ucp[:, c, 2 * t:2 * t + 2, k:k + 256], k == 0, k == 30,
                       ["dg", "ucp"], [psn])
                act(stg[c][:, tsl(t)], ps[:], AF.Identity, [psn, "cvec"], ["stg%d" % c], bias=cb(c))
        for t in range(NT):
            mean_ps, mean_n = pbank[6], "pb6"
            msq_ps, msq_n = pbank[5], "pb5"
            for c in range(2):
                cvb, cvbn = next_mb()
                act(cvb[:], stg[c][:, tsl(t)], AF.Copy, ["stg%d" % c], [cvbn])
                sq, sqn = next_mb()
                act(sq[:], stg[c][:, tsl(t)], AF.Square, ["stg%d" % c], [sqn])
                mm(mean_ps[:], ones_bf[:], cvb[:], c == 0, c == 1, [cvbn, "ones"], [mean_n])
                mm(msq_ps[:], ones_bf[:], sq[:], c == 0, c == 1, [sqn, "ones"], [msq_n])
            mean, meann = next_mt()
            act(mean[:], mean_ps[:], AF.Copy, [mean_n], [meann], scale=1.0 / 256)
            m2, m2n = next_mt()
            tt("pool", m2[:], mean[:], mean[:], ALU.mult, [meann], [m2n])
            var, varn = next_mt()
            stt("dve", var[:], msq_ps[:], 1.0 / 256, m2[:], ALU.mult, ALU.subtract, [msq_n, m2n], [varn])
            act(var[:], var[:], AF.Sqrt, [varn], [varn], bias=EPS)
            rs, rsn = next_mt()
            p.op("dve", lambda h, rs=rs, var=var: h.reciprocal(out=rs[:], in_=var[:]), [varn], [rsn])
            for c in range(2):
                cv, cvn = next_mt()
                tt("pool", cv[:], stg[c][:, tsl(t)], mean[:], ALU.subtract, ["stg%d" % c, meann], [cvn])
                tt("dve", cv[:], cv[:], rs[:], ALU.mult, [cvn, rsn], [cvn])
                act(mixed[:, c, tsl(t)], cv[:], AF.Silu, [cvn, "cvec"], [mn(c, t)], bias=lb(c), scale=lg(c))


    def mixer_s5(l):
        TWO_PI = 6.283185
        S = lambda j: spw[:, j, :]
        are = s5p[:, l * 48:l * 48 + 16]
        aim = s5p[:, l * 48 + 16:l * 48 + 32]
        ldt = s5p[:, l * 48 + 32:l * 48 + 48]
        R, W = ["spw"], ["spw"]
        act(S(0), ldt, AF.Exp, ["s5p"], W)
        tt("dve", S(1), are, S(0), ALU.mult, R + ["s5p"], W)
        tt("dve", S(2), aim, S(0), ALU.mult, R + ["s5p"], W)
        act(S(3), S(1), AF.Exp, R, W)

        def sincos(dst, shift):
            ts("dve", S(4), S(2), 1.0 / TWO_PI, shift, ALU.mult, ALU.add, R, W)
            copy("dve", spi[:], S(4), R, ["spi"])
            copy("dve", S(5), spi[:], ["spi"], W)
            tt("dve", S(4), S(4), S(5), ALU.subtract, R, W)
            ts("dve", S(5), S(4), 0.5, None, ALU.is_gt, ALU.bypass, R, W)
            tt("dve", S(4), S(4), S(5), ALU.subtract, R, W)
            ts("dve", S(5), S(4), -0.5, None, ALU.is_lt, ALU.bypass, R, W)
            tt("dve", S(4), S(4), S(5), ALU.add, R, W)
            act(dst, S(4), AF.Sin, R, W, scale=TWO_PI)

        sincos(S(6), 0.0)
        sincos(S(7), 0.25)
        tt("dve", S(8), S(3), S(7), ALU.mult, R, W)
        tt("dve", S(9), S(3), S(6), ALU.mult, R, W)
        ts("dve", S(10), S(8), -1.0, None, ALU.add, ALU.bypass, R, W)
        tt("dve", S(11), are, are, ALU.mult, R + ["s5p"], W)
        tt("dve", S(12), aim, aim, ALU.mult, R + ["s5p"], W)
        tt("dve", S(11), S(11), S(12), ALU.add, R, W)
        p.op("dve", lambda h: h.reciprocal(out=S(11), in_=S(11)), R, W)
        tt("dve", S(12), S(10), are, ALU.mult, R + ["s5p"], W)
        tt("dve", S(13), S(9), aim, ALU.mult, R + ["s5p"], W)
        tt("dve", S(12), S(12), S(13), ALU.add, R, W)
        tt("dve", S(13), S(9), are, ALU.mult, R + ["s5p"], W)
        tt("dve", S(14), S(10), aim, ALU.mult, R + ["s5p"], W)
        tt("dve", S(13), S(13), S(14), ALU.subtract, R, W)
        tt("dve", S(14), S(12), S(11), ALU.mult, R, W)
        tt("dve", S(15), S(13), S(11), ALU.mult, R, W)

        wb0 = wbuf[0]
        BT = wb0[:, 0:2048].rearrange("p (r d f q n) -> p r d f q n", r=2, d=2, f=2, q=2)
        CT = wb0[:, 2048:4096].rearrange("p (r i n) -> p r i n", r=2, i=16)
        for hf in range(4):
            dma("pool", wb0[:, hf * 512:(hf + 1) * 512], s5BT_d[l][:, hf * 512:(hf + 1) * 512], [], ["wbuf0"])
            dma("pool", wb0[:, 2048 + hf * 512:2048 + (hf + 1) * 512], s5CT_d[l][:, hf * 512:(hf + 1) * 512], [], ["wbuf0"])
        dma("pool", gluW[:], s5glu_d[l].rearrange("(k p) n -> p k n", p=128), [], ["gluW"])
        ub = stg[0].bitcast(BF16)[:, :].rearrange("p (f t) -> p f t", f=2)
        ygb = stg[1].bitcast(BF16)[:, :].rearrange("p (f t) -> p f t", f=2)
        for fc in range(2):
            dma("pool", ub[:, fc, :], zscr[2304 + fc * 128:2304 + (fc + 1) * 128, :], ["zscr9_%d" % fc], ["stg0"])
        dgf = dg.bitcast(F32)
        QW, PW = 64, 8
        Qt = {(0, 0): (tmp[0], "tmp0"), (0, 1): (tmp[1], "tmp1"), (1, 0): (tmp[2], "tmp2"), (1, 1): (tmp[3], "tmp3")}
        Pre_o, Pim_o, SCR_o = 0, 128, 256
        DG = ["dg"]

        class TB:
            def __init__(self, t, off, pstep, W, names):
                self.t, self.off, self.pstep, self.W, self.names = t, off, pstep, W, names

            def tab(self, i0, n, c0, c1):
                return bass.AP(self.t, self.off + i0 * self.W + c0, [[self.pstep, 128], [self.W, n], [1, c1 - c0]])

            def bsc(self, i0, n, pos, width):
                return bass.AP(self.t, self.off + i0 * self.W + pos, [[self.pstep, 128], [self.W, n], [0, width]])

        def scr(k, n, width):
            return bass.AP(dgf, SCR_o + k * 256, [[1984, 128], [width, n], [1, width]])

        def doubling(Tr, Ti, W, i0, rev_):
            Lk = 1
            nm = list(set(Tr.names + Ti.names + DG))
            while Lk < W:
                if not rev_:
                    s0_, d0_, p_ = 0, Lk, Lk - 1
                else:
                    s0_, d0_, p_ = W - Lk, W - 2 * Lk, W - Lk
                src_re, src_im = Tr.tab(i0, 8, s0_, s0_ + Lk), Ti.tab(i0, 8, s0_, s0_ + Lk)
                dst_re, dst_im = Tr.tab(i0, 8, d0_, d0_ + Lk), Ti.tab(i0, 8, d0_, d0_ + Lk)
                cB, sB = Tr.bsc(i0, 8, p_, Lk), Ti.bsc(i0, 8, p_, Lk)
                t1, t2, t3 = scr(0, 8, Lk), scr(1, 8, Lk), scr(2, 8, Lk)
                tt("dve", t1, src_im, sB, ALU.mult, nm, nm)
                tt("dve", t2, src_im, cB, ALU.mult, nm, nm)
                tt("dve", t3, src_re, sB, ALU.mult, nm, nm)
                tt("dve", dst_im, t3, t2, ALU.add, nm, nm)
                tt("dve", t3, src_re, cB, ALU.mult, nm, nm)
                tt("dve", dst_re, t3, t1, ALU.subtract, nm, nm)
                Lk *= 2

        QT = {}
        PT = {}
        for half in range(2):
            rev_ = (half == 1)
            Qr = TB(Qt[(half, 0)][0], 0, 512, QW, [Qt[(half, 0)][1]])
            Qi = TB(Qt[(half, 1)][0], 0, 512, QW, [Qt[(half, 1)][1]])
            Pr = TB(dgf, Pre_o + half * 64, 1984, PW, DG)
            Pi = TB(dgf, Pim_o + half * 64, 1984, PW, DG)
            QT[half] = (Qr, Qi)
            PT[half] = (Pr, Pi)
            i0 = half * 8
            qpos = QW - 1 if rev_ else 0
            copy("dve", Qr.tab(0, 8, qpos, qpos + 1), spw[:, 7, i0:i0 + 8].rearrange("p (i o) -> p i o", o=1), R, Qr.names)
            copy("dve", Qi.tab(0, 8, qpos, qpos + 1), spw[:, 6, i0:i0 + 8].rearrange("p (i o) -> p i o", o=1), R, Qi.names)
            doubling(Qr, Qi, QW, 0, rev_)
            ppos = PW - 1 if rev_ else 0
            qlast = 0 if rev_ else QW - 1
            copy("dve", Pr.tab(0, 8, ppos, ppos + 1), Qr.tab(0, 8, qlast, qlast + 1), Qr.names + DG, DG)
            copy("dve", Pi.tab(0, 8, ppos, ppos + 1), Qi.tab(0, 8, qlast, qlast + 1), Qi.names + DG, DG)
            doubling(Pr, Pi, PW, 0, rev_)
        ucpf = ucp.bitcast(F32)
        setB = [bass.AP(ucpf, i * 512, [[2288, 128], [1, 512]]) for i in range(4)]
        Ere, Eim, Tre, Tim, rmask, nEre, nEim = [mt[i] for i in range(7)]
        g1, G1, g2, G2 = mt[7], mt[8], mt[9], mt[10]
        onesF = mt[11]
        memset("pool", onesF, 1.0, ["mt11"])
        memset("pool", s5o[:], 0.0, ["s5o"])
        s5ov = s5o[:, :].rearrange("p (r i q) -> p r i q", r=2, i=16)
        ybank = 0
        for fc in range(2):
            for t in range(NT):
                memset("dve", pbank[t][:], 0.0, ["pb%d" % t])
            for sc in range(fc * 4, fc * 4 + 4):
                q = sc % 4
                for d in range(2):
                    idx = d * 8 + sc
                    sl = lambda j: spw[:, j, idx:idx + 1]
                    (Qr, Qi), (Pr, Pi) = QT[d], PT[d]
                    qn = Qr.names + Qi.names + DG

                    def pq(Tb, lo, n_a):
                        return bass.AP(Tb.t, Tb.off + sc * PW + lo, [[Tb.pstep, 128], [1, n_a], [0, QW]])

                    def qq(Tb, n_a):
                        return bass.AP(Tb.t, Tb.off + sc * QW, [[Tb.pstep, 128], [0, n_a], [1, QW]])

                    if d == 0:
                        blk = lambda X: X[:, 64:512].rearrange("p (a b) -> p a b", b=QW)
                        qblk = lambda X: X[:, 0:64]
                        plo = 0
                    else:
                        blk = lambda X: X[:, 0:448].rearrange("p (a b) -> p a b", b=QW)
                        qblk = lambda X: X[:, 448:512]
                        plo = 1
                    g1v = g1[:, 0:448].rearrange("p (a b) -> p a b", b=QW)
                    g2v = g2[:, 0:448].rearrange("p (a b) -> p a b", b=QW)
                    tt("dve", blk(Ere), pq(Pr, plo, 7), qq(Qr, 7), ALU.mult, qn, ["mt0"])
                    tt("pool", g1v, pq(Pi, plo, 7), qq(Qi, 7), ALU.mult, qn, ["mt7"])
                    tt("dve", blk(Ere), blk(Ere), g1v, ALU.subtract, ["mt0", "mt7"], ["mt0"])
                    tt("dve", blk(Eim), pq(Pr, plo, 7), qq(Qi, 7), ALU.mult, qn, ["mt1"])
                    tt("pool", g2v, pq(Pi, plo, 7), qq(Qr, 7), ALU.mult, qn, ["mt9"])
                    tt("dve", blk(Eim), blk(Eim), g2v, ALU.add, ["mt1", "mt9"], ["mt1"])
                    act(qblk(Ere), bass.AP(Qr.t, Qr.off + sc * QW, [[Qr.pstep, 128], [1, QW]]), AF.Copy, qn, ["mt0"])
                    act(qblk(Eim), bass.AP(Qi.t, Qi.off + sc * QW, [[Qi.pstep, 128], [1, QW]]), AF.Copy, qn, ["mt1"])
                    ts("pool", nEre, Ere, -1.0, 0.0, ALU.mult, ALU.add, ["mt0"], ["mt5"])
                    ts("pool", nEim, Eim, -1.0, 0.0, ALU.mult, ALU.add, ["mt1"], ["mt6"])
                    act(g1, Eim, AF.Copy, ["mt1"] + R, ["mt7"], scale=sl(15))
                    stt("dve", Tre, Ere, sl(14), g1, ALU.mult, ALU.add, ["mt0", "mt7"] + R, ["mt2"])
                    act(g2, Eim, AF.Copy, ["mt1"] + R, ["mt9"], scale=sl(14))
                    stt("dve", Tim, Ere, sl(15), g2, ALU.mult, ALU.subtract, ["mt0", "mt9"] + R, ["mt3"])
                    act(rmask, Ere, AF.Identity, ["mt0"] + R, ["mt4"], scale=0.0, bias=sl(3))
                    bcol = 256 if d == 0 else 255
                    ts("dve", rmask[:, bcol:bcol + 1], rmask[:, bcol:bcol + 1], mcar, None, ALU.mult, ALU.bypass, ["mt4", "consts"], ["mt4"])
                    ivr = s5init[:, l * 32 + idx:l * 32 + idx + 1]
                    ivi = s5init[:, l * 32 + 16 + idx:l * 32 + 16 + idx + 1]
                    order = list(range(NT)) if d == 0 else list(range(NT - 1, -1, -1))
                    first = True
                    pending = [None]
                    later = []
                    for t in order:
                        if ybank % 2 == 0:
                            cg1, cG1, cg2, cG2 = g1, G1, g2, G2
                            n_g1, n_G1, n_g2, n_G2 = "mt7", "mt8", "mt9", "mt10"
                        else:
                            cg1, cG1, cg2, cG2 = setB
                            n_g1, n_G1, n_g2, n_G2 = "uA0", "uA1", "uA2", "uA3"
                        bre, bren = pbank[4 + 2 * (ybank % 2)], "pb%d" % (4 + 2 * (ybank % 2))
                        bim, bimn = pbank[5 + 2 * (ybank % 2)], "pb%d" % (5 + 2 * (ybank % 2))
                        ybank += 1
                        hs = slice(64 * (q // 2), 64 * (q // 2) + 64)
                        mm(bre[:], BT[hs, 0, d, fc, q % 2, :], ub[hs, fc, tsl(t)], True, True, ["wbuf0", "stg0"], [bren])
                        mm(bim[:], BT[hs, 1, d, fc, q % 2, :], ub[hs, fc, tsl(t)], True, True, ["wbuf0", "stg0"], [bimn])
                        V = lambda a: a
                        tt("dve", cg1, bre[:], V(Tre), ALU.mult, [bren, "mt2"], [n_g1])
                        tt("dve", cG1, bim[:], V(Tim), ALU.mult, [bimn, "mt3"], [n_G1])
                        tt("dve", cg1, cg1, cG1, ALU.subtract, [n_g1, n_G1], [n_g1])
                        tt("dve", cg2, bre[:], V(Tim), ALU.mult, [bren, "mt3"], [n_g2])
                        tt("dve", cG2, bim[:], V(Tre), ALU.mult, [bimn, "mt2"], [n_G2])
                        tt("dve", cg2, cg2, cG2, ALU.add, [n_g2, n_G2], [n_g2])
                        inr = ivr if first else s5c[:, 0:1]
                        ini = ivi if first else s5c[:, 1:2]
                        first = False
                        if d == 0:
                            p.op("dve", lambda h, inr=inr, o_=cG1, i_=cg1: h.tensor_tensor_scan(out=o_, data0=rmask, data1=i_, initial=inr, op0=ALU.mult, op1=ALU.add),
                                 ["mt4", n_g1, "s5c", "s5init"], [n_G1])
                            p.op("dve", lambda h, ini=ini, o_=cG2, i_=cg2: h.tensor_tensor_scan(out=o_, data0=rmask, data1=i_, initial=ini, op0=ALU.mult, op1=ALU.add),
                                 ["mt4", n_g2, "s5c", "s5init"], [n_G2])
                        else:
                            p.op("dve", lambda h, inr=inr, o_=cG1, i_=cg1: h.tensor_tensor_scan(out=rev(o_, 512), data0=rev(rmask, 512), data1=rev(i_, 512), initial=inr,
                                                                                op0=ALU.mult, op1=ALU.add), ["mt4", n_g1, "s5c", "s5init"], [n_G1])
                            p.op("dve", lambda h, ini=ini, o_=cG2, i_=cg2: h.tensor_tensor_scan(out=rev(o_, 512), data0=rev(rmask, 512), data1=rev(i_, 512), initial=ini,
                                                                                op0=ALU.mult, op1=ALU.add), ["mt4", n_g2, "s5c", "s5init"], [n_G2])
                        def make_stage_b(cG1=cG1, cG2=cG2, n_G1=n_G1, n_G2=n_G2, t=t, hs=hs, idx=idx):
                            def stage_b():
                                prods = []
                                for (Gx, Gn, Ex, En) in [(cG1, n_G1, Ere, "mt0"), (cG1, n_G1, nEim, "mt6"), (cG2, n_G2, nEim, "mt6"), (cG2, n_G2, nEre, "mt5")]:
                                    pb_, pbn = next_mb()
                                    tt("pool", pb_, Gx, Ex, ALU.mult, [Gn, En], [pbn])
                                    prods.append((pb_, pbn))
                                for j, (pb_, pbn) in enumerate(prods):
                                    mm(pbank[t][hs, :], CT[:, j % 2, idx, :], pb_, False, True, ["wbuf0", pbn], ["pb%d" % t], sgc=True)
                            return stage_b
                        if pending[0] is not None:
                            later.append(pending[0])
                        pending[0] = make_stage_b()
                        c0 = 255 if d == 0 else 0
                        for jj in range(2):
                            cc_ = c0 + 256 * jj
                            col = lambda a: a[:, cc_:cc_ + 1]
                            act(sm[:, 48 + jj:49 + jj], col(cG1), AF.Copy, [n_G1, "mt0"], ["smx%d" % jj], scale=col(Ere))
                            act(s5ov[:, 0, idx, 2 * t + jj:2 * t + jj + 1], col(cG2), AF.Identity, [n_G2, "mt6", "smx%d" % jj], ["s5o"],
                                scale=col(nEim), bias=sm[:, 48 + jj:49 + jj])
                            act(sm[:, 52 + jj:53 + jj], col(cG1), AF.Copy, [n_G1, "mt1"], ["smz%d" % jj], scale=col(Eim))
                            act(s5ov[:, 1, idx, 2 * t + jj:2 * t + jj + 1], col(cG2), AF.Identity, [n_G2, "mt0", "smz%d" % jj], ["s5o"],
                                scale=col(Ere), bias=sm[:, 52 + jj:53 + jj])
                        cj = 2 * t + (1 if d == 0 else 0)
                        act(s5c[:, 0:1], s5ov[:, 0, idx, cj:cj + 1], AF.Copy, ["s5o", "consts"], ["s5c"], scale=mcar)
                        act(s5c[:, 1:2], s5ov[:, 1, idx, cj:cj + 1], AF.Copy, ["s5o", "consts"], ["s5c"], scale=mcar)
                        while later:
                            later.pop(0)()
                    if pending[0] is not None:
                        pending[0]()
                        pending[0] = None
            for t in range(NT):
                dma(next_q(), g1, zscr[2304 + fc * 128:2304 + (fc + 1) * 128, tsl(t)], ["zscr9_%d" % fc], ["mt7"])
                stt("dve", G1, g1, s5vec[:, l * 4 + fc:l * 4 + fc + 1], pbank[t][:], ALU.mult, ALU.add, ["mt7", "s5vec", "pb%d" % t], ["mt8"])
                tt("pool", g2, G1, G1, ALU.mult, ["mt8"], ["mt9"])
                ts("pool", g2, g2, 0.044715, 1.0, ALU.mult, ALU.add, ["mt9"], ["mt9"])
                tt("pool", g2, g2, G1, ALU.mult, ["mt9", "mt8"], ["mt9"])
                act(G2, g2, AF.Sigmoid, ["mt9"], ["mt10"], scale=1.5957691216)
                tt("pool", ygb[:, fc, tsl(t)], G1, G2, ALU.mult, ["mt8", "mt10"], ["stg1"])
        dma("sp", s5st_d[:, l * 256:(l + 1) * 256], s5o[:], ["s5o"], ["s5st_d"])
        for oc in range(2):
            for t in range(NT):
                ps, psn = next_ps()
                for kc in range(2):
                    mm(ps[:], gluW[:, kc, oc * 128:(oc + 1) * 128], ygb[:, kc, tsl(t)], kc == 0, kc == 1, ["gluW", "stg1"], [psn])
                sg, sgn = next_mt()
                act(sg, ps[:], AF.Sigmoid, [psn, "s5vec"], [sgn], bias=s5vec[:, l * 4 + 2 + oc:l * 4 + 3 + oc])
                tt("pool", mixed[:, 6 + oc, tsl(t)], ygb[:, oc, tsl(t)], sg, ALU.mult, ["stg1", sgn], [mn(6 + oc, t)])


    def mixer_hgrn(l):
        pbb = pbank[4].bitcast(BF16)
        A = [mt[i] for i in range(8)]
        An = ["mt%d" % i for i in range(8)]
        Bf = [mb[i] for i in range(6)] + [hrawb[:, 8192 + i * 512:8192 + (i + 1) * 512] for i in range(8)]
        Bn = ["mb%d" % i for i in range(6)] + ["mt%d" % (8 + i // 2) for i in range(8)]
        scm = hrawb[:, 12288 + 2048:12288 + 3072]
        qi, qo, qd, kd, kdec, vb = Bf[0], Bf[1], Bf[2], Bf[3], Bf[6], Bf[7]
        qin, qon, qdn, kdn, kdecn, vbn = Bn[0], Bn[1], Bn[2], Bn[3], Bn[6], Bn[7]
        ktok, ktokn = Bf[8], Bn[8]
        ko = [Bf[9], Bf[10], Bf[11]]
        kon = [Bn[9], Bn[10], Bn[11]]
        e16 = hgs[:, 0:33]
        dec = hgs[:, 36:44]
        so_i = [0]
        tile_ctr = [0]
        for d in range(2):
            for hp in range(2):
                a_ = d * 2 + hp
                lb_ap = hgl[:, 32 + a_ * 4 + l:32 + a_ * 4 + l + 1]
                oml_ap = hgl[:, 48 + a_ * 4 + l:48 + a_ * 4 + l + 1]
                dma("sp", hgS[:], hginit_d[:, ((l * 2 + d) * 2 + hp) * 64:((l * 2 + d) * 2 + hp) * 64 + 64], [], ["hgS"])
                for i in range(3):
                    memset("pool", ko[i], 0.0, [kon[i]])
                memset("dve", hgs[:, 0:33], 0.0, ["e16"])
                torder = list(range(NT)) if d == 0 else list(range(NT - 1, -1, -1))
                zfrow = (1792 if d == 0 else 2048) + hp * 128
                for t in torder:
                    lf, b_, ea, kk, m16 = A[1], A[2], A[3], A[4], A[6]
                    par_ = tile_ctr[0] % 2
                    tile_ctr[0] += 1
                    if par_ == 0:
                        zf, zfn, qs, qsn = tmp[0][:, :], "tmp0", tmp[1][:, :], "tmp1"
                        vb, vbn = sqb[0][:, :], "sqb0"
                    else:
                        zf, zfn, qs, qsn = tmp[2][:, :], "tmp2", tmp[3][:, :], "tmp3"
                        vb, vbn = sqb[1][:, :], "sqb1"
                    dma(next_q(), zf, zscr[zfrow:zfrow + 128, tsl(t)], ["zscr%d_%d" % (7 + d, hp)], [zfn])
                    dma(next_q(), qs, zscr[1024 + hp * 128:1024 + hp * 128 + 128, tsl(t)], ["zscr4_%d" % hp], [qsn])
                    dma("pool", vb.rearrange("p (s n) -> p s n", s=4),
                        ztok[1][t * 512:(t + 1) * 512, hp * 128:(hp + 1) * 128].rearrange("(s p) n -> p s n", p=128), ["ztok1"], [vbn])
                    act(zf, zf, AF.Sigmoid, [zfn], [zfn])
                    act(qs, qs, AF.Silu, [qsn], [qsn])
                    ts("dve", zf, zf, oml_ap, lb_ap, ALU.mult, ALU.add, [zfn, "hgl"], [zfn])
                    ts("dve", kk, zf, -1.0, 1.0, ALU.mult, ALU.add, [zfn], [An[4]])
                    ts("pool", lf, zf, 3.0e38, 1e-30, ALU.min, ALU.max, [zfn], [An[1]])
                    act(lf, lf, AF.Ln, [An[1]], [An[1]])
                    if d == 0:
                        p.op("dve", lambda h: h.tensor_tensor_scan(out=b_, data0=hgcm[:, :], data1=lf, initial=0.0, op0=ALU.mult, op1=ALU.add),
                             [An[1], "hgcm"], [An[2]])
                    else:
                        p.op("dve", lambda h: h.tensor_tensor_scan(out=rev(b_, 512), data0=hgcm[:, :], data1=rev(lf, 512), initial=0.0,
                                                                   op0=ALU.mult, op1=ALU.add), [An[1], "hgcm"], [An[2]])
                    if d == 0:
                        copy("dve", hgs[:, 1:33], cols(b_, 15, 16, 32), [An[2]], ["e16"])
                        rb = bass.AP(hgs, 0, [[48, 128], [1, 32], [0, 16]])
                        bend_b = bass.AP(hgs, 4, [[48, 128], [4, 8], [0, 64]])
                        act(dec, cols(hgs[:, 0:33], 4, 4, 8), AF.Exp, ["e16"], ["hgs_d"])
                    else:
                        copy("dve", hgs[:, 0:32], cols(b_, 0, 16, 32), [An[2]], ["e16"])
                        rb = bass.AP(hgs, 1, [[48, 128], [1, 32], [0, 16]])
                        bend_b = bass.AP(hgs, 0, [[48, 128], [4, 8], [0, 64]])
                        act(dec, cols(hgs[:, 0:33], 0, 4, 8), AF.Exp, ["e16"], ["hgs_d"])
                    copy("dve", m16[:, 0:32], cols(b_, 8, 16, 32), [An[2]], [An[6]])
                    mbc = bass.AP(m16.tensor, m16.offset, [[m16.ap[0][0], 128], [1, 32], [0, 16]])
                    v32 = lambda ap: ap.rearrange("p (n k) -> p n k", n=32)
                    v8 = lambda ap: ap.rearrange("p (j k) -> p j k", j=8)
                    A8, A8n = mt[11], "mt11"
                    act(A[0], b_, AF.Exp, [An[2]], [An[0]])
                    tt("pool", qi, qs, A[0], ALU.mult, [qsn, An[0]], [qin])
                    tt("dve", v8(A[1]), bend_b, v8(b_), ALU.subtract, ["e16", An[2]], [An[1]])
                    act(A[1], A[1], AF.Exp, [An[1]], [An[1]])
                    tt("pool", kdec, kk, A[1], ALU.mult, [An[4], An[1]], [kdecn])
                    tt("dve", v32(ea), v32(b_), rb, ALU.subtract, ["e16", An[2]], [An[3]])
                    ts("dve", ea, ea, 0.0, None, ALU.min, ALU.bypass, [An[3]], [An[3]])
                    act(ea, ea, AF.Exp, [An[3]], [An[3]])
                    tt("pool", qo, qs, ea, ALU.mult, [qsn, An[3]], [qon])
                    tt("dve", v32(A[7]), v32(b_), mbc, ALU.subtract, [An[6], An[2]], [An[7]])
                    act(A[6], A[7], AF.Exp, [An[7]], [An[6]])
                    tt("dve", qd, qs, A[6], ALU.mult, [qsn, An[6]], [qdn])
                    act(A8, A[7], AF.Exp, [An[7]], [A8n], scale=-1.0)
                    tt("dve", kd, kk, A8, ALU.mult, [An[4], A8n], [kdn])
                    for i in range(3):
                        if d == 0:
                            w_ = 16 * (i + 1)
                            Rb = bass.AP(hgs, i + 1, [[48, 128], [4, 8], [0, w_]])
                            sl_ = slice(0, w_)
                        else:
                            w_ = 64 - 16 * (i + 1)
                            Rb = bass.AP(hgs, i + 1, [[48, 128], [4, 8], [0, w_]])
                            sl_ = slice(16 * (i + 1), 64)
                        Tk, Tkn = [(A[7], An[7]), (A[0], An[0]), (A[1], An[1])][i]
                        tt("dve", v8(Tk)[:, :, sl_], Rb, v8(b_)[:, :, sl_], ALU.subtract, ["e16", An[2]], [Tkn])
                        act(v8(Tk)[:, :, sl_], v8(Tk)[:, :, sl_], AF.Exp, [Tkn], [Tkn])
                        tt("pool", v8(ko[i])[:, :, sl_], v8(kk)[:, :, sl_], v8(Tk)[:, :, sl_], ALU.mult, [An[4], Tkn], [kon[i]])
                    for sub in range(4):
                        p.op("pe", lambda h, sub=sub: h.transpose(out=pbb[:, sub * 128:(sub + 1) * 128], in_=kdec[:, sub * 128:(sub + 1) * 128],
                                                                   identity=identb[:]), [kdecn, "identb"], ["pb4"])
                    act(ktok, pbb[:, 0:512], AF.Copy, ["pb4"], [ktokn])
                    for bk in (0, 1, 6, 7):
                        memset("dve", pbank[bk][:], 0.0, ["pb%d" % bk])
                    for j in range(8):
                        par = j % 2
                        cs = slice(64 * j, 64 * j + 64)
                        for h2 in range(2):
                            ks = slice(64 * h2, 64 * h2 + 64)
                            mm(pbank[h2][64 * par:64 * par + 64, j * 64:j * 64 + 64], kd[ks, cs], qd[ks, cs], True, True, [kdn, qdn], ["pb%d" % h2])
                            for i in range(3):
                                I = i + 1 if d == 0 else i
                                tcs = slice(64 * j + 16 * I, 64 * j + 16 * I + 16)
                                mm(pbank[6 + h2][64 * par:64 * par + 64, j * 64 + 16 * I:j * 64 + 16 * I + 16], ko[i][ks, cs], qo[ks, tcs], True, True,
                                   [kon[i], qon], ["pb%d" % (6 + h2)])
                    mk = bass.AP(hgmask, 64 * d, [[128, 128], [0, 8], [1, 64]])
                    for h2 in range(2):
                        tt("dve", v8(A[7]), v8(pbank[h2][:]), mk, ALU.mult, ["pb%d" % h2, "hgmask"], [An[7]])
                        tt("dve", scm[:, h2 * 512:(h2 + 1) * 512], A[7], pbank[6 + h2][:], ALU.add, [An[7], "pb%d" % (6 + h2)], ["mb4", "mb5"])
                    for j in range(8):
                        par, sub = j % 2, j // 2
                        ts_ = slice(64 * par, 64 * par + 64)
                        ub_ = 3 if par == 0 else 5
                        for h2 in range(2):
                            mm(pbank[ub_][64 * h2:64 * h2 + 64, j * 64:(j + 1) * 64], ktok[ts_, sub * 128 + 64 * h2:sub * 128 + 64 * h2 + 64],
                               vb[ts_, sub * 128 + 64 * h2:sub * 128 + 64 * h2 + 64], True, True, [ktokn, vbn], ["pb%d" % ub_])
                    memset("dve", pbank[2][:], 0.0, ["pb2"])
                    jorder = list(range(8)) if d == 0 else list(range(7, -1, -1))
                    for j in jorder:
                        par, sub = j % 2, j // 2
                        cs = slice(64 * j, 64 * j + 64)
                        J = 8 * t + j
                        first = (J == 0) if d == 0 else (J == 31)
                        seq_start = (J % 4 == 0) if d == 0 else (J % 4 == 3)
                        seq_end = (J % 4 == 3) if d == 0 else (J % 4 == 0)
                        if seq_start and not first:
                            ts("dve", hgS[:], hgS[:], mcar, None, ALU.mult, ALU.bypass, ["hgS", "consts"], ["hgS"])
                        for h2 in range(2):
                            ks = slice(64 * h2, 64 * h2 + 64)
                            copy("dve", hgSb[ks, 64 * h2:64 * h2 + 64], hgS[ks, :], ["hgS"], ["hgSb"])
                        for h2 in range(2):
                            ks = slice(64 * h2, 64 * h2 + 64)
                            mm(pbank[2][ks, cs], vb[:, sub * 128 + 64 * h2:sub * 128 + 64 * h2 + 64], scm[:, (h2 * 8 + j) * 64:(h2 * 8 + j) * 64 + 64],
                               False, True, [vbn, "mb4", "mb5"], ["pb2"], sgc=True)
                        mm(pbank[2][:, cs], hgSb[:, :], qi[:, cs], False, True, ["hgSb", qin], ["pb2"], sgc=True)
                        ub_ = 3 if par == 0 else 5
                        stt("dve", hgS[:], hgS[:], dec[:, j:j + 1], pbank[ub_][:, j * 64:(j + 1) * 64], ALU.mult, ALU.add,
                            ["pb%d" % ub_, "hgs_d", "hgS"], ["hgS"])
                        if seq_end:
                            so, son = hgSo[so_i[0] % 2], "hgSo%d" % (so_i[0] % 2)
                            so_i[0] += 1
                            copy("pool", so[:], hgS[:], ["hgS"], [son])
                            off = ((((l * 2 + d) * 2 + hp) * 8) + J // 4) * 64
                            dma("sp", hgst_d[:, off:off + 64], so[:], [son], ["hgst_d"])
                    o_, gz, t1 = A[5], A[1], A[3]
                    if d == 0:
                        act(o_, pbank[2][:], AF.Copy, ["pb2"], [An[5]])
                        dma(next_q(), oscr[hp * 128:(hp + 1) * 128, tsl(t)], o_, [An[5]], ["oscr%d" % hp])
                    else:
                        dma(next_q(), o_, oscr[hp * 128:(hp + 1) * 128, tsl(t)], ["oscr%d" % hp], [An[5]])
                        tt("dve", o_, o_, pbank[2][:], ALU.add, [An[5], "pb2"], [An[5]])
                        act(Bf[2], o_, AF.Square, [An[5]], [Bn[2]])
                        mm(pbank[4][:], hgblk[:], Bf[2], True, True, ["hgblk", Bn[2]], ["pb4"])
                        act(t1, pbank[4][:], AF.Sqrt, ["pb4"], [An[3]], bias=EPS)
                        p.op("dve", lambda h: h.reciprocal(out=t1, in_=t1), [An[3]], [An[3]])
                        dma(next_q(), gz, zscr[1536 + hp * 128:1536 + hp * 128 + 128, tsl(t)], ["zscr6_%d" % hp], [An[1]])
                        act(gz, gz, AF.Silu, [An[1]], [An[1]])
                        tt("dve", o_, o_, t1, ALU.mult, [An[5], An[3]], [An[5]])
                        stt("dve", mixed[:, 4 + hp, tsl(t)], o_, hgng[:, l * 2 + hp:l * 2 + hp + 1], gz, ALU.mult, ALU.mult,
                            [An[5], "hgng", An[1]], [mn(4 + hp, t)])

    for l in range(nlayers):
        b0 = l * 48
        cur_l[0] = l
        norm_mod(lambda k: gm1[:, l * 8 + k:l * 8 + k + 1], lambda k: adasb[:, b0 + k:b0 + k + 1], "gm1_%d" % l,
                 lambda k, t: (hb[:, k, tsl(t)], hn(k, t)))
        for blk in range(10):
            w_, wn = next_w()
            w_ = w_[:, 0:2048].rearrange("p (k n) -> p k n", k=8)
            dma("pool", w_, w_in[l][:, blk * 256:(blk + 1) * 256].rearrange("(k p) n -> p k n", p=128), [], [wn])
            if blk in (3, 5):
                zi = 0 if blk == 3 else 1
                for half in range(2):
                    s_, sn = next_stg()
                    for j in range(8):
                        t16 = half * 8 + j
                        ps, psn = next_ps()
                        for k in range(8):
                            mm(ps[:, 0:256], hb[:, k, t16 * 128:(t16 + 1) * 128], w_[:, k, 0:256], k == 0, k == 7,
                               [wn, hn(k, t16 // 4)], [psn])
                        act(s_[:, j * 256:(j + 1) * 256], ps[:, 0:256], AF.Copy, [psn], [sn])
                    dma(next_q(), ztok[zi][half * 1024:(half + 1) * 1024, :].rearrange("(j p) n -> p j n", p=128),
                        s_[:].rearrange("p (j n) -> p j n", n=256), [sn], ["ztok%d" % zi])
                continue
            for c in range(2):
                s_, sn = next_stg()
                for t in range(NT):
                    ps, psn = next_ps()
                    for k in range(8):
                        mm(ps[:], w_[:, k, c * 128:(c + 1) * 128], hb[:, k, tsl(t)], k == 0, k == 7, [wn, hn(k, t)], [psn])
                    act(s_[:, tsl(t)], ps[:], AF.Copy, [psn], [sn])
                dma(next_q(), zscr[blk * 256 + c * 128: blk * 256 + (c + 1) * 128, :], s_[:], [sn], ["zscr%d_%d" % (blk, c)])

        p.barrier()
        zero_chunks = []
        if not mix["conv"]:
            zero_chunks += [0, 1]
        if not mix["gmlp"]:
            zero_chunks += [2, 3]
        if not mix["hgrn"]:
            zero_chunks += [4, 5]
        if not mix["s5"]:
            zero_chunks += [6, 7]
        for k in zero_chunks:
            memset("pool", mixed[:, k, :], 0.0, [mn(k, t) for t in range(NT)])
        if mix["gmlp"]:
            mixer_gmlp(l)
        if mix["conv"]:
            mixer_conv(l)
        if mix["s5"]:
            p.barrier()
            mixer_s5(l)
        if mix["hgrn"]:
            mixer_hgrn(l)

        for oc in range(8):
            if oc % 4 == 0:
                w_, wn = next_w()
                w_ = w_[:, :].rearrange("p (k n) -> p k n", k=8)
                dma("pool", w_, w_out[l][:, (oc // 4) * 512:(oc // 4 + 1) * 512].rearrange("(k p) n -> p k n", p=128), [], [wn])
            for t in range(NT):
                ps, psn = next_ps()
                for k in range(8):
                    mm(ps[:], w_[:, k, (oc % 4) * 128:(oc % 4 + 1) * 128], mixed[:, k, tsl(t)], k == 0, k == 7, [wn, mn(k, t)], [psn])
                stt("dve", x[:, oc, tsl(t)], ps[:], adasb[:, b0 + 16 + oc:b0 + 17 + oc], x[:, oc, tsl(t)], ALU.mult, ALU.add,
                    [psn, "ada%d" % l, xn(oc, t)], [xn(oc, t)])
        p.barrier()
        norm_mod(lambda k: gm2[:, l * 8 + k:l * 8 + k + 1], lambda k: adasb[:, b0 + 24 + k:b0 + 25 + k], "gm2_%d" % l,
                 lambda k, t: (hb[:, k, tsl(t)], hn(k, t)))
        for hbk in range(8):
            w1, w1n = next_w()
            w1 = w1[:, :].rearrange("p (k n) -> p k n", k=8)
            dma("pool", w1, mlp_w1[l][:, hbk * 512:(hbk + 1) * 512].rearrange("(k p) n -> p k n", p=128), [], [w1n])
            w2, w2n = next_w()
            w2 = w2[:, :].rearrange("p (c n) -> p c n", c=4)
            dma("pool", w2, mlp_w2[l][hbk * 512:(hbk + 1) * 512, :].rearrange("(c p) n -> p c n", p=128), [], [w2n])
            for hc in range(4):
                for t in range(NT):
                    ps, psn = next_ps()
                    for k in range(8):
                        mm(ps[:], w1[:, k, hc * 128:(hc + 1) * 128], hb[:, k, tsl(t)], k == 0, k == 7, [w1n, hn(k, t)], [psn])
                    tm, tmn = next_tmp()
                    act(tm[:], ps[:], AF.Relu, [psn], [tmn])
                    tt("pool", mixed[:, hc, tsl(t)], tm[:], tm[:], ALU.mult, [tmn], [mn(hc, t)])
            for oc in range(8):
                for t in range(NT):
                    ps, psn = next_ps()
                    for hc in range(4):
                        mm(ps[:], w2[:, hc, oc * 128:(oc + 1) * 128], mixed[:, hc, tsl(t)], hc == 0, hc == 3, [w2n, mn(hc, t)], [psn])
                    stt("dve", x[:, oc, tsl(t)], ps[:], adasb[:, b0 + 40 + oc:b0 + 41 + oc], x[:, oc, tsl(t)], ALU.mult, ALU.add,
                        [psn, "ada%d" % l, xn(oc, t)], [xn(oc, t)])

    fin = []

    def fin_out(k, t):
        return stg[k % 2][:, tsl(t)], "stg%d" % (k % 2)

    for t in range(NT):
        ps, psn = pbank[4 + (t % 2)], "pb%d" % (4 + (t % 2))
        for k in range(8):
            sq, sqn = next_sq()
            act(sq[:], x[:, k, tsl(t)], AF.Square, [xn(k, t)], [sqn])
            mm(ps[:], ones_bf[:], sq[:], k == 0, k == 7, [sqn, "ones"], [psn])
        sd, sdn = next_tmp()
        act(sd[:], ps[:], AF.Sqrt, [psn], [sdn], bias=EPS, scale=1.0 / D)
        rs, rsn = next_tmp()
        p.op("dve", lambda h, rs=rs, sd=sd: h.reciprocal(out=rs[:], in_=sd[:]), [sdn], [rsn])
        for k in range(8):
            tm, tmn = next_tmp()
            if tmn == rsn:
                tm, tmn = next_tmp()
            stt("dve", tm[:], x[:, k, tsl(t)], fng[:, k:k + 1], rs[:], ALU.mult, ALU.mult, [xn(k, t), rsn, "fng"], [tmn])
            fin.append(dma(next_q(), yout[k * 128:(k + 1) * 128, tsl(t)], tm[:], [tmn], []))

    o = p.op("sp", lambda h: h.nop())
    o.deps = fin
    for f in fin:
        f.sig = True
    p.emit()
    st.close()
    return nc


_NC_CACHE = {}


def kernel(**inp):
    f = lambda a: np.ascontiguousarray(np.asarray(a, dtype=np.float32))
    x_prompt = f(inp["x_prompt"])
    x_sample = f(inp["x_sample"])
    key = "main"
    if key not in _NC_CACHE:
        _NC_CACHE[key] = build_program()
    nc = _NC_CACHE[key]
    shared = {k: f(inp[k]) for k in ["ada_w", "w_in", "w_out", "mlp_w1", "mlp_w2"]}
    fm = lambda a, n: np.ascontiguousarray(f(a).reshape(-1, n, 128).transpose(2, 0, 1).reshape(128, -1))
    shared["norm1_gT"] = fm(inp["norm1_g"], 8)
    shared["norm2_gT"] = fm(inp["norm2_g"], 8)
    shared["ada_bT"] = fm(inp["ada_b"], 48)
    shared["final_norm_gT"] = fm(inp["final_norm_g"], 8)
    shared["gmlp_norm_g"] = f(inp["gmlp_norm_g"])
    shared["gmlp_wsT"] = np.ascontiguousarray(f(inp["gmlp_ws"]).transpose(0, 1, 3, 2))
    shared["gmlp_bs"] = f(inp["gmlp_bs"])
    cw = f(inp["conv_w"])
    shared["conv_wT"] = np.ascontiguousarray(cw.reshape(L, 31, 2, 128).transpose(3, 0, 2, 1).reshape(128, L * 62))
    cvv = np.stack([f(inp["conv_b"]), f(inp["conv_ln_g"]), f(inp["conv_ln_b"])], axis=1)
    shared["conv_vecT"] = np.ascontiguousarray(cvv.reshape(L, 3, 2, 128).transpose(3, 0, 1, 2).reshape(128, L * 6))
    shared["ident"] = np.eye(128, dtype=np.float32)
    def st_lay(a):
        a = f(a)
        lead = a.shape[:-2]
        return a.reshape(lead + (8, 128)).transpose((len(lead) + 1,) + tuple(range(len(lead))) + (len(lead),))
    ldt_e = np.repeat(f(inp["s5_log_dt"])[..., None], 64, axis=-1)
    s5p = np.stack([st_lay(inp["s5_a_re"]), st_lay(inp["s5_a_im"]), st_lay(ldt_e)], axis=2)
    shared["s5pT"] = np.ascontiguousarray(s5p.reshape(128, L * 48))
    BTb = np.zeros((L, 128, 2, 2, 2, 2, 128), np.float32)
    CTb = np.zeros((L, 128, 2, 16, 64), np.float32)
    for ri, (bb, cc) in enumerate([(f(inp["s5_b_re"]), f(inp["s5_c_re"])), (f(inp["s5_b_im"]), f(inp["s5_c_im"]))]):
        for d_ in range(2):
            for g in range(16):
                q = (g // 2) % 4
                BTb[:, (g % 8) * 16:(g % 8) * 16 + 16, ri, d_, g // 8, q % 2, (g % 2) * 64:(g % 2) * 64 + 64] = bb[:, d_, g].transpose(0, 2, 1)
                CTb[:, (g % 2) * 64:(g % 2) * 64 + 64, ri, d_ * 8 + g // 2, 32 * (q % 2) + 16 * (g % 2):32 * (q % 2) + 16 * (g % 2) + 16] = cc[:, d_, g].transpose(0, 2, 1)
    shared["s5_BTblk"] = np.ascontiguousarray(BTb.reshape(L, 128, 2048))
    shared["s5_CTblk"] = np.ascontiguousarray(CTb.reshape(L, 128, 2048))
    sv = np.stack([f(inp["s5_d"]), f(inp["s5_glu_b"])], axis=1)
    shared["s5vecT"] = np.ascontiguousarray(sv.reshape(L, 2, 2, 128).transpose(3, 0, 1, 2).reshape(128, L * 4))
    shared["s5_glu_w"] = f(inp["s5_glu_w"])
    s5i_zero = np.zeros((128, L * 32), np.float32)
    lbl = f(inp["hgrn_lb_logits"])
    shared["hglbT"] = np.ascontiguousarray(lbl.reshape(2, L, 2, 128).transpose(3, 0, 2, 1).reshape(128, 16))
    shared["hgngT"] = np.ascontiguousarray(f(inp["hgrn_norm_g"]).reshape(L, 2, 128).transpose(2, 0, 1).reshape(128, L * 2))
    ii = np.arange(64)
    same = (ii[:, None] // 16) == (ii[None, :] // 16)
    mf = ((ii[:, None] <= ii[None, :]) & same).astype(np.float32)
    mbk = ((ii[:, None] >= ii[None, :]) & same).astype(np.float32)
    shared["hgmask"] = np.ascontiguousarray(np.concatenate([np.concatenate([mf, mbk], axis=1)] * 2, axis=0))
    cmk = np.ones((128, 512), np.float32)
    cmk[:, ::64] = 0.0
    shared["hgcmask"] = cmk
    blk = np.zeros((128, 128), np.float32)
    blk[:64, :64] = 1.0 / 64
    blk[64:, 64:] = 1.0 / 64
    shared["hgblk"] = blk
    hgi_zero = np.zeros((128, L * 256), np.float32)
    def hg_lay(a):
        a = f(a).reshape(L, 2, 2, 2, 64, 64)
        return np.ascontiguousarray(a.transpose(3, 4, 0, 1, 2, 5).reshape(128, L * 256))
    in_maps = []
    for c in range(8):
        m = dict(shared)
        if c < 4:
            m["xin"] = np.ascontiguousarray(x_prompt[8 * c:8 * c + 8].reshape(T, D).T)
            m["cond"] = fm(inp["c_ctx"], 8)
            m["mcar"] = np.zeros((128, 1), np.float32)
            m["s5init"] = s5i_zero
            m["hginit"] = hgi_zero
        else:
            m["xin"] = np.ascontiguousarray(x_sample[c - 4].T)
            m["cond"] = fm(inp["c"][c - 4], 8)
            m["mcar"] = np.ones((128, 1), np.float32)
            b = c - 4
            si = np.stack([st_lay(inp["state_s5_re"][b]), st_lay(inp["state_s5_im"][b])], axis=2)
            m["s5init"] = np.ascontiguousarray(si.reshape(128, L * 32))
            m["hginit"] = hg_lay(inp["state_hgrn"][b])
        in_maps.append(m)
    res = run_bass_kernel_spmd(nc, in_maps, core_ids=list(range(8)))
    r = res.results
    y_prompt = np.stack([r[c]["yout"].T.reshape(8, 256, D) for c in range(4)]).reshape(32, 256, D)
    y_sample = np.stack([r[c]["yout"].T for c in range(4, 8)])
    def hg_unlay(c):
        a = r[c]["hgst"].reshape(2, 64, L, 2, 2, 8, 64)
        return a.transpose(5, 2, 3, 4, 0, 1, 6).reshape(8, L, 2, 4, 64, 64)
    hg = np.ascontiguousarray(np.concatenate([hg_unlay(c) for c in range(4)], axis=0))
    def s5_unlay(c):
        a = r[c]["s5st"].reshape(2, 64, L, 2, 2, 8, 8)
        return a.transpose(3, 6, 2, 4, 5, 0, 1).reshape(2, 8, L, 2, 16, 64)
    s5all = np.concatenate([s5_unlay(c) for c in range(4)], axis=1)
    s5r = np.ascontiguousarray(s5all[0])
    s5i = np.ascontiguousarray(s5all[1])
    return (np.ascontiguousarray(y_prompt), np.ascontiguousarray(y_sample), hg, s5r, s5i)
```

```python
import math
import numpy as np
import concourse.bass as bass
import concourse.mybir as mybir
from concourse.bass_utils import run_bass_kernel_spmd
from contextlib import ExitStack

F32 = mybir.dt.float32
BF16 = mybir.dt.bfloat16
AF = mybir.ActivationFunctionType
ALU = mybir.AluOpType
AX = mybir.AxisListType

NDS = 40
L = 4
D = 1024
T = 2048
NT = 4
TT = 512
EPS = 1e-6
MIX = {"conv": True, "gmlp": True, "hgrn": True, "s5": True}
NLAYERS = L


class Buf:
    __slots__ = ("name", "w", "r")

    def __init__(self, name):
        self.name = name
        self.w = None
        self.r = []


class Op:
    __slots__ = ("eng", "idx", "fn", "deps", "is_dma", "sig", "sem", "val")

    def __init__(self, eng, idx, fn, is_dma):
        self.eng = eng
        self.idx = idx
        self.fn = fn
        self.is_dma = is_dma
        self.deps = []
        self.sig = False
        self.sem = None
        self.val = 0


class Prog:
    ENGS = ["pe", "act", "dve", "pool", "sp"]

    def __init__(self, nc, stack):
        self.nc = nc
        self.ops = {e: [] for e in self.ENGS}
        self.ndma = 0
        self.dma_last = [None] * NDS
        self.dma_cnt = [0] * NDS
        self.esem = {e: stack.enter_context(nc.semaphore("es_" + e)) for e in self.ENGS}
        self.dsem = [stack.enter_context(nc.semaphore("ds%d" % i)) for i in range(NDS)]
        self.bufs = {}

    def _tb(self, x):
        if isinstance(x, Buf):
            return x
        b = self.bufs.get(x)
        if b is None:
            b = Buf(x)
            self.bufs[x] = b
        return b

    def op(self, eng, fn, reads=(), writes=(), dma=False):
        o = Op(eng, len(self.ops[eng]), fn, dma)
        deps = {}
        reads = [self._tb(b) for b in reads]
        writes = [self._tb(b) for b in writes]

        def add(p, raw):
            if p is None:
                return
            if p.eng == eng and (not p.is_dma) and (not dma):
                if eng == "pe" or not raw:
                    return
            deps[id(p)] = p

        for b in reads:
            add(b.w, True)
        for b in writes:
            add(b.w, False)
            for r in b.r:
                add(r, False)
        if dma:
            k = self.ndma % NDS
            self.ndma += 1
            prev = self.dma_last[k]
            if prev is not None:
                deps[id(prev)] = prev
            self.dma_cnt[k] += 1
            o.sem = self.dsem[k]
            o.val = 16 * self.dma_cnt[k]
            o.sig = True
            self.dma_last[k] = o
        o.deps = list(deps.values())
        for p in o.deps:
            p.sig = True
        for b in reads:
            b.r.append(o)
        for b in writes:
            b.w = o
            b.r = []
        self.ops[eng].append(o)
        return o

    def barrier(self):
        lasts = [self.ops[e][-1] for e in self.ENGS if self.ops[e]] + [d for d in self.dma_last if d is not None]
        for e in self.ENGS:
            o = self.op(e, lambda h: h.nop())
            o.deps = [q for q in lasts if q.is_dma or q.eng != e]
            for q in o.deps:
                q.sig = True

    def emit(self):
        nc = self.nc
        for e in self.ENGS:
            c = 0
            for o in self.ops[e]:
                if o.is_dma:
                    continue
                if o.sig:
                    c += 1
                    o.sem = self.esem[e]
                    o.val = c
        ops = self.ops

        def run(e, h):
            known = {}
            for o in ops[e]:
                for p in o.deps:
                    key = id(p.sem)
                    if known.get(key, 0) >= p.val:
                        continue
                    h.wait_ge(p.sem, p.val)
                    known[key] = p.val
                ins = o.fn(h)
                if o.sig:
                    ins.then_inc(o.sem, 16 if o.is_dma else 1)

        with nc.Block() as block:
            @block.tensor
            def _(h):
                run("pe", h)

            @block.scalar
            def _(h):
                run("act", h)

            @block.vector
            def _(h):
                run("dve", h)

            @block.gpsimd
            def _(h):
                run("pool", h)

            @block.sync
            def _(h):
                run("sp", h)


def build_program(nlayers=NLAYERS, mix=MIX):
    nc = bass.Bass("TRN2", target_bir_lowering=False)
    st = ExitStack()
    p = Prog(nc, st)

    def din(name, shape):
        return nc.dram_tensor(name, list(shape), F32, kind="ExternalInput").ap()

    def dout(name, shape):
        return nc.dram_tensor(name, list(shape), F32, kind="ExternalOutput").ap()

    xin = din("xin", [D, T])
    cond = din("cond", [128, 8])
    mcar_d = din("mcar", [128, 1])
    norm1_g = din("norm1_gT", [128, L * 8])
    norm2_g = din("norm2_gT", [128, L * 8])
    ada_w = din("ada_w", [L, D, 6 * D])
    ada_b = din("ada_bT", [128, L * 48])
    w_in = din("w_in", [L, D, 2560])
    w_out = din("w_out", [L, D, D])
    mlp_w1 = din("mlp_w1", [L, D, 4 * D])
    mlp_w2 = din("mlp_w2", [L, 4 * D, D])
    final_g = din("final_norm_gT", [128, 8])
    gm_ng = din("gmlp_norm_g", [L, 256])
    gm_wsT = din("gmlp_wsT", [L, 4, 128, 128])
    gm_bs = din("gmlp_bs", [L, 4, 128])
    cv_w = din("conv_wT", [128, L * 2 * 31])
    cv_vec = din("conv_vecT", [128, L * 3 * 2])
    ident_d = din("ident", [128, 128])
    s5p_d = din("s5pT", [128, L * 48])
    s5BT_d = din("s5_BTblk", [L, 128, 2048])
    s5CT_d = din("s5_CTblk", [L, 128, 2048])
    s5vec_d = din("s5vecT", [128, L * 4])
    s5glu_d = din("s5_glu_w", [L, 256, 256])
    s5init_d = din("s5init", [128, L * 32])
    s5st_d = dout("s5st", [128, L * 256])
    hglb_d = din("hglbT", [128, 16])
    hgng_d = din("hgngT", [128, L * 2])
    hginit_d = din("hginit", [128, L * 256])
    hgmask_d = din("hgmask", [128, 128])
    hgcm_d = din("hgcmask", [128, 512])
    hgblk_d = din("hgblk", [128, 128])
    hgst_d = dout("hgst", [128, L * 2048])
    oscr = nc.dram_tensor("oscr", [256, T], F32).ap()
    ztok = [nc.dram_tensor("ztok%d" % i, [T, 256], F32).ap() for i in range(2)]
    yout = dout("yout", [D, T])
    zscr = nc.dram_tensor("zscr", [2560, T], F32).ap()

    def sb(name, shape, dt=F32):
        return st.enter_context(nc.sbuf_tensor("sb_" + name, list(shape), dt))

    x = sb("x", [128, 8, T])
    hraw = sb("hraw", [128, 8192])
    hb = hraw.bitcast(BF16)[:, :].rearrange("p (k t) -> p k t", k=8)
    hrawb = hraw.bitcast(BF16)
    mixed = sb("mixed", [128, 8, T], BF16)
    wbuf = [sb("wbuf%d" % i, [128, 4096], BF16) for i in range(2)]
    stg = [sb("stg%d" % i, [128, T]) for i in range(2)]
    tmp = [sb("tmp%d" % i, [128, TT]) for i in range(4)]
    sqb = [sb("sqb%d" % i, [128, TT], BF16) for i in range(3)]
    ones_bf = sb("ones_bf", [128, 128], BF16)
    consts = sb("consts", [128, 64])
    adasb = sb("adasb", [128, L * 48])
    adab = sb("adab", [128, L * 48])
    n1g = sb("n1g", [128, L * 8])
    n2g = sb("n2g", [128, L * 8])
    fng = sb("fng", [128, 8])
    gm1 = sb("gm1", [128, L * 8])
    gm2 = sb("gm2", [128, L * 8])
    csb = sb("csb", [128, 8])
    csbf = sb("csbf", [128, 8], BF16)
    zero8 = sb("zero8", [128, 8])
    mt = [hraw[:, i * 512:(i + 1) * 512] for i in range(12)]
    mb = [hrawb[:, 12288 + i * 512:12288 + (i + 1) * 512] for i in range(6)]
    ucp = sb("ucp", [128, 2, 8, 286], BF16)
    ident = sb("ident_sb", [128, 128])
    gB = sb("gB", [128, 256])
    wsT = sb("wsT", [128, 4, 128], BF16)
    bsB = sb("bsB", [128, 2, 128])
    cwt = sb("cwt", [128, L * 62])
    cvec = sb("cvec", [128, L * 6])
    dg = sb("dg", [128, 31, 128], BF16)
    sm = sb("sm", [128, 64])
    s5p = sb("s5p", [128, L * 48])
    s5vec = sb("s5vec", [128, L * 4])
    s5init = sb("s5init", [128, L * 32])
    s5o = sb("s5o", [128, 256])
    spw = sb("spw", [128, 20, 16])
    spi = sb("spi", [128, 16], mybir.dt.int32)
    s5c = sb("s5c", [128, 16])
    gluW = sb("gluW", [128, 2, 256], BF16)
    hgl = sb("hgl", [128, 64])
    hgng = sb("hgng", [128, L * 2])
    hgmask = sb("hgmask", [128, 128])
    hgcm = sb("hgcm", [128, 512])
    hgblk = sb("hgblk", [128, 128], BF16)
    identb = sb("identb", [128, 128], BF16)
    hgs = sb("hgs", [128, 48])
    hgS = sb("hgS", [128, 64])
    hgSo = [sb("hgSo%d" % i, [128, 64]) for i in range(2)]
    hgSb = sb("hgSb", [128, 128], BF16)
    pbank = [st.enter_context(nc.psum_tensor("pb%d" % i, [128, 512], F32)) for i in range(8)]

    def dma(q, out, in_, reads, writes):
        return p.op(q, lambda h: h.dma_start(out=out, in_=in_), reads, writes, dma=True)

    def act(out, in_, func, reads, writes, bias=0.0, scale=1.0):
        return p.op("act", lambda h: h.activation(out=out, in_=in_, func=func, bias=bias, scale=scale), reads, writes)

    def tt(eng, out, a, b, op, reads, writes):
        return p.op(eng, lambda h: h.tensor_tensor(out=out, in0=a, in1=b, op=op), reads, writes)

    def ts(eng, out, a, s1, s2, op0, op1, reads, writes):
        return p.op(eng, lambda h: h.tensor_scalar(out=out, in0=a, scalar1=s1, scalar2=s2, op0=op0, op1=op1), reads, writes)

    def stt(eng, out, a, s, b, op0, op1, reads, writes):
        return p.op(eng, lambda h: h.scalar_tensor_tensor(out=out, in0=a, scalar=s, in1=b, op0=op0, op1=op1), reads, writes)

    def mm(out, lhsT, rhs, start, stop, reads, writes, sgc=False):
        return p.op("pe", lambda h: h.matmul(out, lhsT, rhs, start=start, stop=stop, skip_group_check=sgc), reads, writes)

    def rev(ap, n):
        return bass.AP(ap.tensor, ap.offset + (n - 1), [[ap.ap[0][0], 128], [-1, n]])

    def cols(ap, c0, step, n):
        return bass.AP(ap.tensor, ap.offset + c0, [[ap.ap[0][0], 128], [step, n]])

    def copy(eng, out, in_, reads, writes):
        return p.op(eng, lambda h: h.tensor_copy(out=out, in_=in_), reads, writes)

    def memset(eng, ap, val, writes):
        return p.op(eng, lambda h: h.memset(ap, val), (), writes)

    cnt = {"ps": 0, "tmp": 0, "sq": 0, "w": 0, "stg": 0, "q": 0, "mt": 0, "mb": 0}

    def next_mt():
        i = cnt["mt"] % 12
        cnt["mt"] += 1
        return mt[i], "mt%d" % i

    def next_mb():
        i = cnt["mb"] % 6
        cnt["mb"] += 1
        return mb[i], "mb%d" % i

    def bcast_rows(ap, off, nrows, ncols):
        return bass.AP(ap.tensor, ap.offset + off, [[0, nrows], [1, ncols]])

    def next_ps():
        i = cnt["ps"] % 4
        cnt["ps"] += 1
        return pbank[i], "pb%d" % i

    def next_tmp():
        i = cnt["tmp"] % 4
        cnt["tmp"] += 1
        return tmp[i], "tmp%d" % i

    def next_sq():
        i = cnt["sq"] % 3
        cnt["sq"] += 1
        return sqb[i], "sqb%d" % i

    def next_w():
        i = cnt["w"] % 2
        cnt["w"] += 1
        return wbuf[i], "wbuf%d" % i

    def next_stg():
        i = cnt["stg"] % 2
        cnt["stg"] += 1
        return stg[i], "stg%d" % i

    def next_q():
        cnt["q"] += 1
        return ["sp", "act"][cnt["q"] % 2]

    def xn(k, t):
        return "x_%d_%d" % (k, t)

    def hn(k, t):
        return "h_%d_%d" % (k, t)

    def mn(k, t):
        return "m_%d_%d" % (k, t)

    tsl = lambda t: slice(t * TT, (t + 1) * TT)

    memset("dve", ones_bf[:], 1.0, ["ones"])
    memset("dve", zero8[:], 0.0, ["zero8"])
    dma("sp", adab[:], ada_b, [], ["adab"])
    dma("sp", n1g[:], norm1_g, [], ["n1g"])
    dma("sp", n2g[:], norm2_g, [], ["n2g"])
    dma("sp", fng[:], final_g, [], ["fng"])
    dma("sp", csb[:], cond, [], ["csb"])
    dma("sp", consts[:, 0:1], mcar_d, [], ["consts"])
    mcar = consts[:, 0:1]
    dma("sp", ident[:], ident_d, [], ["ident"])
    dma("sp", cwt[:], cv_w, [], ["cwt"])
    dma("sp", cvec[:], cv_vec, [], ["cvec"])
    memset("pool", ucp[:], 0.0, ["ucp"])
    dma("sp", s5p[:], s5p_d, [], ["s5p"])
    dma("sp", hgl[:, 0:16], hglb_d, [], ["hgl"])
    memset("pool", hgSb[:], 0.0, ["hgSb"])
    dma("sp", hgng[:], hgng_d, [], ["hgng"])
    dma("sp", hgmask[:], hgmask_d, [], ["hgmask"])
    dma("sp", hgcm[:], hgcm_d, [], ["hgcm"])
    dma("pool", hgblk[:], hgblk_d, [], ["hgblk"])
    dma("pool", identb[:], ident_d, [], ["identb"])
    act(hgl[:, 16:32], hgl[:, 0:16], AF.Exp, ["hgl"], ["hgl"])
    p.op("dve", lambda h: h.tensor_reduce(out=hgs[:, 40:44], in_=hgl[:, 16:32].rearrange("p (a l) -> p a l", l=4), axis=AX.X, op=ALU.add),
         ["hgl"], ["hgs"])
    p.op("dve", lambda h: h.reciprocal(out=hgs[:, 40:44], in_=hgs[:, 40:44]), ["hgs"], ["hgs"])
    tt("dve", hgl[:, 16:32].rearrange("p (a l) -> p a l", l=4), hgl[:, 16:32].rearrange("p (a l) -> p a l", l=4),
       bass.AP(hgs, 40, [[48, 128], [1, 4], [0, 4]]), ALU.mult, ["hgl", "hgs"], ["hgl"])
    lbv = hgl[:, 32:48].rearrange("p (a l) -> p a l", l=4)
    smv = hgl[:, 16:32].rearrange("p (a l) -> p a l", l=4)
    memset("dve", hgl[:, 32:48], 0.0, ["hgl"])
    for li in range(1, 4):
        tt("dve", lbv[:, :, li:li + 1], lbv[:, :, li - 1:li], smv[:, :, li:li + 1], ALU.add, ["hgl"], ["hgl"])
    ts("dve", hgl[:, 48:64], hgl[:, 32:48], -1.0, 1.0, ALU.mult, ALU.add, ["hgl"], ["hgl"])
    dma("sp", s5vec[:], s5vec_d, [], ["s5vec"])
    dma("sp", s5init[:], s5init_d, [], ["s5init"])

    for k in range(8):
        dma(next_q(), x[:, k, :], xin[k * 128:(k + 1) * 128, :], [], [xn(k, t) for t in range(NT)])
    act(csbf[:], csb[:], AF.Silu, ["csb"], ["csbf"])
    def ada_block(l, cb):
        w_, wn = next_w()
        w_ = w_[:, :].rearrange("p (k n) -> p k n", k=8)
        dma("pool", w_, ada_w[l][:, cb * 512:(cb + 1) * 512].rearrange("(k p) n -> p k n", p=128), [], [wn])
        for j in range(4):
            col = l * 48 + cb * 4 + j
            for k in range(8):
                mm(pbank[7][:, col:col + 1], w_[:, k, j * 128:(j + 1) * 128], csbf[:, k:k + 1], k == 0, k == 7,
                   [wn, "csbf"], ["pb7_%d" % l])

    abuf = sb("abuf", [128, 8, 64])
    csb32 = sb("csb32", [128, 8])
    act(csb32[:], csb[:], AF.Silu, ["csb"], ["csb32"])

    def ada_small(l, jj):
        dma("sp", abuf[:], ada_w[l][:, jj * 64:(jj + 1) * 64].rearrange("(k p) n -> p k n", p=128), [], ["abuf"])
        col = l * 48 + jj // 2
        ps_ = slice(64 * (jj % 2), 64 * (jj % 2) + 64)
        for k in range(8):
            mm(pbank[7][ps_, col:col + 1], abuf[:, k, :], csb32[:, k:k + 1], k == 0, k == 7, ["abuf", "csb32"], ["pb7_%d" % l])

    def ada_finish(l):
        b0 = l * 48
        tt("dve", adasb[:, b0:b0 + 48], pbank[7][:, b0:b0 + 48], adab[:, b0:b0 + 48], ALU.add, ["pb7_%d" % l, "adab"], ["ada%d" % l])
        stt("dve", gm1[:, l * 8:(l + 1) * 8], adasb[:, b0 + 8:b0 + 16], 1.0, n1g[:, l * 8:(l + 1) * 8], ALU.add, ALU.mult,
            ["ada%d" % l, "n1g"], ["gm1_%d" % l])
        stt("dve", gm2[:, l * 8:(l + 1) * 8], adasb[:, b0 + 32:b0 + 40], 1.0, n2g[:, l * 8:(l + 1) * 8], ALU.add, ALU.mult,
            ["ada%d" % l, "n2g"], ["gm2_%d" % l])

    I32 = mybir.dt.int32
    rowf, colf = stg[0], stg[1]
    p.op("pool", lambda h: h.iota(rowf[:, :], pattern=[[1, 32], [0, 64]], base=0, channel_multiplier=0, allow_small_or_imprecise_dtypes=True),
         (), ["stg0"])
    p.op("pool", lambda h: h.iota(colf[:, :], pattern=[[0, 32], [1, 64]], base=0, channel_multiplier=0, allow_small_or_imprecise_dtypes=True),
         (), ["stg1"])
    p.op("pool", lambda h: h.iota(consts[:, 8:10], pattern=[[128, 2]], base=0, channel_multiplier=1, allow_small_or_imprecise_dtypes=True),
         (), ["cfq"])
    for l_ in range(nlayers):
        for cb in range(12):
            ada_block(l_, cb)
    act(consts[:, 10:12], consts[:, 8:10], AF.Exp, ["cfq"], ["cfq2"], scale=-math.log(10000.0) / 256.0)
    ts("dve", consts[:, 12:14], consts[:, 10:12], 1.0 / (2.0 * math.pi), None, ALU.mult, ALU.bypass, ["cfq2"], ["cfq3"])
    for k in range(8):
        pv, pvn = (rowf, "stg0") if k < 4 else (colf, "stg1")
        kk = k % 2
        ph = 0.25 if (k // 2) % 2 == 1 else 0.0
        for t in range(NT):
            Y, Yn = next_mt()
            K, Kn = next_mt()
            M, Mn = next_mt()
            act(Y, pv[:, tsl(t)], AF.Identity, [pvn, "cfq3"], [Yn], scale=consts[:, 12 + kk:13 + kk], bias=ph)
            copy("dve", K.bitcast(I32), Y, [Yn], [Kn])
            copy("dve", M, K.bitcast(I32), [Kn], [Mn])
            tt("dve", Y, Y, M, ALU.subtract, [Yn, Mn], [Yn])
            ts("dve", M, Y, 0.5, None, ALU.is_gt, ALU.bypass, [Yn], [Mn])
            tt("dve", Y, Y, M, ALU.subtract, [Yn, Mn], [Yn])
            ts("dve", M, Y, -0.5, None, ALU.is_lt, ALU.bypass, [Yn], [Mn])
            tt("dve", Y, Y, M, ALU.add, [Yn, Mn], [Yn])
            act(Y, Y, AF.Sin, [Yn], [Yn], scale=6.283185)
            stt("dve", x[:, k, tsl(t)], Y, mcar, x[:, k, tsl(t)], ALU.mult, ALU.add, [Yn, "consts", xn(k, t)], [xn(k, t)])
    p.barrier()

    for l_ in range(nlayers):
        ada_finish(l_)

    cur_l = [0]

    def norm_mod(gm_ap, sh_ap, gname, out_fn):
        for t in range(NT):
            ps, psn = pbank[4 + (t % 2)], "pb%d" % (4 + (t % 2))
            for k in range(8):
                sq, sqn = next_sq()
                act(sq[:], x[:, k, tsl(t)], AF.Square, [xn(k, t)], [sqn])
                mm(ps[:], ones_bf[:], sq[:], k == 0, k == 7, [sqn, "ones"], [psn])
            sd, sdn = next_tmp()
            act(sd[:], ps[:], AF.Sqrt, [psn], [sdn], bias=EPS, scale=1.0 / D)
            rs, rsn = next_tmp()
            p.op("dve", lambda h, rs=rs, sd=sd: h.reciprocal(out=rs[:], in_=sd[:]), [sdn], [rsn])
            for k in range(8):
                tm, tmn = next_sq() if False else next_tmp()
                if tmn == rsn:
                    tm, tmn = next_tmp()
                tt("dve", tm[:], x[:, k, tsl(t)], rs[:], ALU.mult, [xn(k, t), rsn], [tmn])
                o_ap, o_n = out_fn(k, t)
                act(o_ap, tm[:], AF.Identity, [tmn, gname, "ada%d" % cur_l[0]], [o_n], bias=sh_ap(k), scale=gm_ap(k))


    def mixer_gmlp(l):
        dma("sp", gB[:], bcast_rows(gm_ng, l * 256, 128, 256), [], ["gB"])
        dma("pool", wsT[:], gm_wsT[l].rearrange("h s t -> s h t"), [], ["wsT"])
        for h in range(4):
            dma("sp", bsB[(h % 2) * 64:(h % 2) * 64 + 64, h // 2, :], bcast_rows(gm_bs, (l * 4 + h) * 128, 64, 128), [], ["bsB"])
        for t in range(NT):
            s_, sn = next_stg()
            vt = s_[:, 0:1024].rearrange("p (j n) -> p j n", n=256)
            dma(next_q(), vt, ztok[0][t * 512:(t + 1) * 512, :].rearrange("(j p) n -> p j n", p=128), ["ztok0"], [sn])
            junk, jn = next_mb()
            for j in range(4):
                p.op("act", lambda h, j=j, junk=junk, vt=vt, t=t: h.activation(out=junk[:, 0:256], in_=vt[:, j, :], func=AF.Square,
                                                                        accum_out=sm[:, t * 4 + j:t * 4 + j + 1]), [sn], [jn, "sm_g%d" % t])
            act(sm[:, 16 + t * 4:16 + t * 4 + 4], sm[:, t * 4:t * 4 + 4], AF.Sqrt, ["sm_g%d" % t], ["sm_h%d" % t], bias=EPS, scale=1.0 / 256)
            p.op("dve", lambda h, t=t: h.reciprocal(out=sm[:, 32 + t * 4:32 + t * 4 + 4], in_=sm[:, 16 + t * 4:16 + t * 4 + 4]),
                 ["sm_h%d" % t], ["sm_r%d" % t])
            vh, vhn = next_mb()
            vh2, vh2n = next_mb()
            vhs = [vh, vh2]
            for j in range(4):
                stt("dve", vhs[j // 2][:, (j % 2) * 256:(j % 2) * 256 + 256], vt[:, j, :], sm[:, 32 + t * 4 + j:32 + t * 4 + j + 1], gB[:],
                    ALU.mult, ALU.mult, [sn, "sm_r%d" % t, "gB"], [vhn if j < 2 else vh2n])
            for c in range(2):
                ps, psn = next_ps()
                for j in range(4):
                    for h2 in range(2):
                        hh = 2 * c + h2
                        src = vhs[j // 2][:, (j % 2) * 256 + hh * 64:(j % 2) * 256 + hh * 64 + 64]
                        mm(ps[h2 * 64:(h2 + 1) * 64, j * 128:(j + 1) * 128], src, wsT[:, hh, :], True, True,
                           [vhn, vh2n, "wsT"], [psn])
                svb, svn = next_mt()
                bb = bass.AP(bsB, c * 128, [[256, 128], [0, 4], [1, 128]])
                tt("dve", svb[:].rearrange("p (j t) -> p j t", j=4), ps[:].rearrange("p (j t) -> p j t", j=4), bb, ALU.add,
                   [psn, "bsB"], [svn])
                ut, utn = next_mt()
                dma(next_q(), ut[:], zscr[512 + c * 128:512 + (c + 1) * 128, tsl(t)], ["zscr2_%d" % c], [utn])
                tt("pool", mixed[:, 2 + c, tsl(t)], ut[:], svb[:], ALU.mult, [utn, svn], [mn(2 + c, t)])

    def mixer_conv(l):
        memset("pool", ucp[:], 0.0, ["ucp"])
        cb = lambda c: cvec[:, l * 6 + c:l * 6 + c + 1]
        lg = lambda c: cvec[:, l * 6 + 2 + c:l * 6 + 3 + c]
        lb = lambda c: cvec[:, l * 6 + 4 + c:l * 6 + 5 + c]
        for c in range(2):
            for t in range(NT):
                va, van = next_mt()
                ga, gan = next_mt()
                dma(next_q(), va[:], zscr[0 + c * 128:0 + (c + 1) * 128, tsl(t)], ["zscr0_%d" % c], [van])
                dma(next_q(), ga[:], zscr[256 + c * 128:256 + (c + 1) * 128, tsl(t)], ["zscr1_%d" % c], [gan])
                act(ga[:], ga[:], AF.Sigmoid, [gan], [gan])
                tt("dve", ucp[:, c, 2 * t:2 * t + 2, 15:271], va[:].rearrange("p (s n) -> p s n", s=2),
                   ga[:].rearrange("p (s n) -> p s n", s=2), ALU.mult, [van, gan], ["ucp"])
            ts("dve", ucp[:, c, 1:8, 0:15], ucp[:, c, 0:7, 256:271], mcar, None, ALU.mult, ALU.bypass, ["ucp", "consts"], ["ucp"])
            ts("dve", ucp[:, c, 0:7, 271:286], ucp[:, c, 1:8, 15:30], mcar, None, ALU.mult, ALU.bypass, ["ucp", "consts"], ["ucp"])
        for c in range(2):
            for k in range(31):
                ts("pool", dg[:, k, :], ident[:], cwt[:, l * 62 + c * 31 + k:l * 62 + c * 31 + k + 1], 1.0, ALU.mult, ALU.mult,
                   ["ident", "cwt"], ["dg"])
            for t in range(NT):
                ps, psn = next_ps()
                for k in range(31):
                    mm(ps[:].rearrange("p (s n) -> p s n", s=2), dg[:, k, :], ucp[:, c, 2 * t:2 * t + 2, k:k + 256], k == 0, k == 30,
                       ["dg", "ucp"], [psn])
                act(stg[c][:, tsl(t)], ps[:], AF.Identity, [psn, "cvec"], ["stg%d" % c], bias=cb(c))
        for t in range(NT):
            mean_ps, mean_n = pbank[6], "pb6"
            msq_ps, msq_n = pbank[5], "pb5"
            for c in range(2):
                cvb, cvbn = next_mb()
                act(cvb[:], stg[c][:, tsl(t)], AF.Copy, ["stg%d" % c], [cvbn])
                sq, sqn = next_mb()
                act(sq[:], stg[c][:, tsl(t)], AF.Square, ["stg%d" % c], [sqn])
                mm(mean_ps[:], ones_bf[:], cvb[:], c == 0, c == 1, [cvbn, "ones"], [mean_n])
                mm(msq_ps[:], ones_bf[:], sq[:], c == 0, c == 1, [sqn, "ones"], [msq_n])
            mean, meann = next_mt()
            act(mean[:], mean_ps[:], AF.Copy, [mean_n], [meann], scale=1.0 / 256)
            m2, m2n = next_mt()
            tt("pool", m2[:], mean[:], mean[:], ALU.mult, [meann], [m2n])
            var, varn = next_mt()
            stt("dve", var[:], msq_ps[:], 1.0 / 256, m2[:], ALU.mult, ALU.subtract, [msq_n, m2n], [varn])
            act(var[:], var[:], AF.Sqrt, [varn], [varn], bias=EPS)
            rs, rsn = next_mt()
            p.op("dve", lambda h, rs=rs, var=var: h.reciprocal(out=rs[:], in_=var[:]), [varn], [rsn])
            for c in range(2):
                cv, cvn = next_mt()
                tt("pool", cv[:], stg[c][:, tsl(t)], mean[:], ALU.subtract, ["stg%d" % c, meann], [cvn])
                tt("dve", cv[:], cv[:], rs[:], ALU.mult, [cvn, rsn], [cvn])
                act(mixed[:, c, tsl(t)], cv[:], AF.Silu, [cvn, "cvec"], [mn(c, t)], bias=lb(c), scale=lg(c))


    def mixer_s5(l):
        TWO_PI = 6.283185
        S = lambda j: spw[:, j, :]
        are = s5p[:, l * 48:l * 48 + 16]
        aim = s5p[:, l * 48 + 16:l * 48 + 32]
        ldt = s5p[:, l * 48 + 32:l * 48 + 48]
        R, W = ["spw"], ["spw"]
        act(S(0), ldt, AF.Exp, ["s5p"], W)
        tt("dve", S(1), are, S(0), ALU.mult, R + ["s5p"], W)
        tt("dve", S(2), aim, S(0), ALU.mult, R + ["s5p"], W)
        act(S(3), S(1), AF.Exp, R, W)

        def sincos(dst, shift):
            ts("dve", S(4), S(2), 1.0 / TWO_PI, shift, ALU.mult, ALU.add, R, W)
            copy("dve", spi[:], S(4), R, ["spi"])
            copy("dve", S(5), spi[:], ["spi"], W)
            tt("dve", S(4), S(4), S(5), ALU.subtract, R, W)
            ts("dve", S(5), S(4), 0.5, None, ALU.is_gt, ALU.bypass, R, W)
            tt("dve", S(4), S(4), S(5), ALU.subtract, R, W)
            ts("dve", S(5), S(4), -0.5, None, ALU.is_lt, ALU.bypass, R, W)
            tt("dve", S(4), S(4), S(5), ALU.add, R, W)
            act(dst, S(4), AF.Sin, R, W, scale=TWO_PI)

        sincos(S(6), 0.0)
        sincos(S(7), 0.25)
        tt("dve", S(8), S(3), S(7), ALU.mult, R, W)
        tt("dve", S(9), S(3), S(6), ALU.mult, R, W)
        ts("dve", S(10), S(8), -1.0, None, ALU.add, ALU.bypass, R, W)
        tt("dve", S(11), are, are, ALU.mult, R + ["s5p"], W)
        tt("dve", S(12), aim, aim, ALU.mult, R + ["s5p"], W)
        tt("dve", S(11), S(11), S(12), ALU.add, R, W)
        p.op("dve", lambda h: h.reciprocal(out=S(11), in_=S(11)), R, W)
        tt("dve", S(12), S(10), are, ALU.mult, R + ["s5p"], W)
        tt("dve", S(13), S(9), aim, ALU.mult, R + ["s5p"], W)
        tt("dve", S(12), S(12), S(13), ALU.add, R, W)
        tt("dve", S(13), S(9), are, ALU.mult, R + ["s5p"], W)
        tt("dve", S(14), S(10), aim, ALU.mult, R + ["s5p"], W)
        tt("dve", S(13), S(13), S(14), ALU.subtract, R, W)
        tt("dve", S(14), S(12), S(11), ALU.mult, R, W)
        tt("dve", S(15), S(13), S(11), ALU.mult, R, W)

        wb0 = wbuf[0]
        BT = wb0[:, 0:2048].rearrange("p (r d f q n) -> p r d f q n", r=2, d=2, f=2, q=2)
        CT = wb0[:, 2048:4096].rearrange("p (r i n) -> p r i n", r=2, i=16)
        for hf in range(4):
            dma("pool", wb0[:, hf * 512:(hf + 1) * 512], s5BT_d[l][:, hf * 512:(hf + 1) * 512], [], ["wbuf0"])
            dma("pool", wb0[:, 2048 + hf * 512:2048 + (hf + 1) * 512], s5CT_d[l][:, hf * 512:(hf + 1) * 512], [], ["wbuf0"])
        dma("pool", gluW[:], s5glu_d[l].rearrange("(k p) n -> p k n", p=128), [], ["gluW"])
        ub = stg[0].bitcast(BF16)[:, :].rearrange("p (f t) -> p f t", f=2)
        ygb = stg[1].bitcast(BF16)[:, :].rearrange("p (f t) -> p f t", f=2)
        for fc in range(2):
            dma("pool", ub[:, fc, :], zscr[2304 + fc * 128:2304 + (fc + 1) * 128, :], ["zscr9_%d" % fc], ["stg0"])
        dgf = dg.bitcast(F32)
        QW, PW = 64, 8
        Qt = {(0, 0): (tmp[0], "tmp0"), (0, 1): (tmp[1], "tmp1"), (1, 0): (tmp[2], "tmp2"), (1, 1): (tmp[3], "tmp3")}
        Pre_o, Pim_o, SCR_o = 0, 128, 256
        DG = ["dg"]

        class TB:
            def __init__(self, t, off, pstep, W, names):
                self.t, self.off, self.pstep, self.W, self.names = t, off, pstep, W, names

            def tab(self, i0, n, c0, c1):
                return bass.AP(self.t, self.off + i0 * self.W + c0, [[self.pstep, 128], [self.W, n], [1, c1 - c0]])

            def bsc(self, i0, n, pos, width):
                return bass.AP(self.t, self.off + i0 * self.W + pos, [[self.pstep, 128], [self.W, n], [0, width]])

        def scr(k, n, width):
            return bass.AP(dgf, SCR_o + k * 256, [[1984, 128], [width, n], [1, width]])

        def doubling(Tr, Ti, W, i0, rev_):
            Lk = 1
            nm = list(set(Tr.names + Ti.names + DG))
            while Lk < W:
                if not rev_:
                    s0_, d0_, p_ = 0, Lk, Lk - 1
                else:
                    s0_, d0_, p_ = W - Lk, W - 2 * Lk, W - Lk
                src_re, src_im = Tr.tab(i0, 8, s0_, s0_ + Lk), Ti.tab(i0, 8, s0_, s0_ + Lk)
                dst_re, dst_im = Tr.tab(i0, 8, d0_, d0_ + Lk), Ti.tab(i0, 8, d0_, d0_ + Lk)
                cB, sB = Tr.bsc(i0, 8, p_, Lk), Ti.bsc(i0, 8, p_, Lk)
                t1, t2, t3 = scr(0, 8, Lk), scr(1, 8, Lk), scr(2, 8, Lk)
                tt("dve", t1, src_im, sB, ALU.mult, nm, nm)
                tt("dve", t2, src_im, cB, ALU.mult, nm, nm)
                tt("dve", t3, src_re, sB, ALU.mult, nm, nm)
                tt("dve", dst_im, t3, t2, ALU.add, nm, nm)
                tt("dve", t3, src_re, cB, ALU.mult, nm, nm)
                tt("dve", dst_re, t3, t1, ALU.subtract, nm, nm)
                Lk *= 2

        QT = {}
        PT = {}
        for half in range(2):
            rev_ = (half == 1)
            Qr = TB(Qt[(half, 0)][0], 0, 512, QW, [Qt[(half, 0)][1]])
            Qi = TB(Qt[(half, 1)][0], 0, 512, QW, [Qt[(half, 1)][1]])
            Pr = TB(dgf, Pre_o + half * 64, 1984, PW, DG)
            Pi = TB(dgf, Pim_o + half * 64, 1984, PW, DG)
            QT[half] = (Qr, Qi)
            PT[half] = (Pr, Pi)
            i0 = half * 8
            qpos = QW - 1 if rev_ else 0
            copy("dve", Qr.tab(0, 8, qpos, qpos + 1), spw[:, 7, i0:i0 + 8].rearrange("p (i o) -> p i o", o=1), R, Qr.names)
            copy("dve", Qi.tab(0, 8, qpos, qpos + 1), spw[:, 6, i0:i0 + 8].rearrange("p (i o) -> p i o", o=1), R, Qi.names)
            doubling(Qr, Qi, QW, 0, rev_)
            ppos = PW - 1 if rev_ else 0
            qlast = 0 if rev_ else QW - 1
            copy("dve", Pr.tab(0, 8, ppos, ppos + 1), Qr.tab(0, 8, qlast, qlast + 1), Qr.names + DG, DG)
            copy("dve", Pi.tab(0, 8, ppos, ppos + 1), Qi.tab(0, 8, qlast, qlast + 1), Qi.names + DG, DG)
            doubling(Pr, Pi, PW, 0, rev_)
        ucpf = ucp.bitcast(F32)
        setB = [bass.AP(ucpf, i * 512, [[2288, 128], [1, 512]]) for i in range(4)]
        Ere, Eim, Tre, Tim, rmask, nEre, nEim = [mt[i] for i in range(7)]
        g1, G1, g2, G2 = mt[7], mt[8], mt[9], mt[10]
        onesF = mt[11]
        memset("pool", onesF, 1.0, ["mt11"])
        memset("pool", s5o[:], 0.0, ["s5o"])
        s5ov = s5o[:, :].rearrange("p (r i q) -> p r i q", r=2, i=16)
        ybank = 0
        for fc in range(2):
            for t in range(NT):
                memset("dve", pbank[t][:], 0.0, ["pb%d" % t])
            for sc in range(fc * 4, fc * 4 + 4):
                q = sc % 4
                for d in range(2):
                    idx = d * 8 + sc
                    sl = lambda j: spw[:, j, idx:idx + 1]
                    (Qr, Qi), (Pr, Pi) = QT[d], PT[d]
                    qn = Qr.names + Qi.names + DG

                    def pq(Tb, lo, n_a):
                        return bass.AP(Tb.t, Tb.off + sc * PW + lo, [[Tb.pstep, 128], [1, n_a], [0, QW]])

                    def qq(Tb, n_a):
                        return bass.AP(Tb.t, Tb.off + sc * QW, [[Tb.pstep, 128], [0, n_a], [1, QW]])

                    if d == 0:
                        blk = lambda X: X[:, 64:512].rearrange("p (a b) -> p a b", b=QW)
                        qblk = lambda X: X[:, 0:64]
                        plo = 0
                    else:
                        blk = lambda X: X[:, 0:448].rearrange("p (a b) -> p a b", b=QW)
                        qblk = lambda X: X[:, 448:512]
                        plo = 1
                    g1v = g1[:, 0:448].rearrange("p (a b) -> p a b", b=QW)
                    g2v = g2[:, 0:448].rearrange("p (a b) -> p a b", b=QW)
                    tt("dve", blk(Ere), pq(Pr, plo, 7), qq(Qr, 7), ALU.mult, qn, ["mt0"])
                    tt("pool", g1v, pq(Pi, plo, 7), qq(Qi, 7), ALU.mult, qn, ["mt7"])
                    tt("dve", blk(Ere), blk(Ere), g1v, ALU.subtract, ["mt0", "mt7"], ["mt0"])
                    tt("dve", blk(Eim), pq(Pr, plo, 7), qq(Qi, 7), ALU.mult, qn, ["mt1"])
                    tt("pool", g2v, pq(Pi, plo, 7), qq(Qr, 7), ALU.mult, qn, ["mt9"])
                    tt("dve", blk(Eim), blk(Eim), g2v, ALU.add, ["mt1", "mt9"], ["mt1"])
                    act(qblk(Ere), bass.AP(Qr.t, Qr.off + sc * QW, [[Qr.pstep, 128], [1, QW]]), AF.Copy, qn, ["mt0"])
                    act(qblk(Eim), bass.AP(Qi.t, Qi.off + sc * QW, [[Qi.pstep, 128], [1, QW]]), AF.Copy, qn, ["mt1"])
                    ts("pool", nEre, Ere, -1.0, 0.0, ALU.mult, ALU.add, ["mt0"], ["mt5"])
                    ts("pool", nEim, Eim, -1.0, 0.0, ALU.mult, ALU.add, ["mt1"], ["mt6"])
                    act(g1, Eim, AF.Copy, ["mt1"] + R, ["mt7"], scale=sl(15))
                    stt("dve", Tre, Ere, sl(14), g1, ALU.mult, ALU.add, ["mt0", "mt7"] + R, ["mt2"])
                    act(g2, Eim, AF.Copy, ["mt1"] + R, ["mt9"], scale=sl(14))
                    stt("dve", Tim, Ere, sl(15), g2, ALU.mult, ALU.subtract, ["mt0", "mt9"] + R, ["mt3"])
                    act(rmask, Ere, AF.Identity, ["mt0"] + R, ["mt4"], scale=0.0, bias=sl(3))
                    bcol = 256 if d == 0 else 255
                    ts("dve", rmask[:, bcol:bcol + 1], rmask[:, bcol:bcol + 1], mcar, None, ALU.mult, ALU.bypass, ["mt4", "consts"], ["mt4"])
                    ivr = s5init[:, l * 32 + idx:l * 32 + idx + 1]
                    ivi = s5init[:, l * 32 + 16 + idx:l * 32 + 16 + idx + 1]
                    order = list(range(NT)) if d == 0 else list(range(NT - 1, -1, -1))
                    first = True
                    pending = [None]
                    later = []
                    for t in order:
                        if ybank % 2 == 0:
                            cg1, cG1, cg2, cG2 = g1, G1, g2, G2
                            n_g1, n_G1, n_g2, n_G2 = "mt7", "mt8", "mt9", "mt10"
                        else:
                            cg1, cG1, cg2, cG2 = setB
                            n_g1, n_G1, n_g2, n_G2 = "uA0", "uA1", "uA2", "uA3"
                        bre, bren = pbank[4 + 2 * (ybank % 2)], "pb%d" % (4 + 2 * (ybank % 2))
                        bim, bimn = pbank[5 + 2 * (ybank % 2)], "pb%d" % (5 + 2 * (ybank % 2))
                        ybank += 1
                        hs = slice(64 * (q // 2), 64 * (q // 2) + 64)
                        mm(bre[:], BT[hs, 0, d, fc, q % 2, :], ub[hs, fc, tsl(t)], True, True, ["wbuf0", "stg0"], [bren])
                        mm(bim[:], BT[hs, 1, d, fc, q % 2, :], ub[hs, fc, tsl(t)], True, True, ["wbuf0", "stg0"], [bimn])
                        V = lambda a: a
                        tt("dve", cg1, bre[:], V(Tre), ALU.mult, [bren, "mt2"], [n_g1])
                        tt("dve", cG1, bim[:], V(Tim), ALU.mult, [bimn, "mt3"], [n_G1])
                        tt("dve", cg1, cg1, cG1, ALU.subtract, [n_g1, n_G1], [n_g1])
                        tt("dve", cg2, bre[:], V(Tim), ALU.mult, [bren, "mt3"], [n_g2])
                        tt("dve", cG2, bim[:], V(Tre), ALU.mult, [bimn, "mt2"], [n_G2])
                        tt("dve", cg2, cg2, cG2, ALU.add, [n_g2, n_G2], [n_g2])
                        inr = ivr if first else s5c[:, 0:1]
                        ini = ivi if first else s5c[:, 1:2]
                        first = False
                        if d == 0:
                            p.op("dve", lambda h, inr=inr, o_=cG1, i_=cg1: h.tensor_tensor_scan(out=o_, data0=rmask, data1=i_, initial=inr, op0=ALU.mult, op1=ALU.add),
                                 ["mt4", n_g1, "s5c", "s5init"], [n_G1])
                            p.op("dve", lambda h, ini=ini, o_=cG2, i_=cg2: h.tensor_tensor_scan(out=o_, data0=rmask, data1=i_, initial=ini, op0=ALU.mult, op1=ALU.add),
                                 ["mt4", n_g2, "s5c", "s5init"], [n_G2])
                        else:
                            p.op("dve", lambda h, inr=inr, o_=cG1, i_=cg1: h.tensor_tensor_scan(out=rev(o_, 512), data0=rev(rmask, 512), data1=rev(i_, 512), initial=inr,
                                                                                op0=ALU.mult, op1=ALU.add), ["mt4", n_g1, "s5c", "s5init"], [n_G1])
                            p.op("dve", lambda h, ini=ini, o_=cG2, i_=cg2: h.tensor_tensor_scan(out=rev(o_, 512), data0=rev(rmask, 512), data1=rev(i_, 512), initial=ini,
                                                                                op0=ALU.mult, op1=ALU.add), ["mt4", n_g2, "s5c", "s5init"], [n_G2])
                        def make_stage_b(cG1=cG1, cG2=cG2, n_G1=n_G1, n_G2=n_G2, t=t, hs=hs, idx=idx):
                            def stage_b():
                                prods = []
                                for (Gx, Gn, Ex, En) in [(cG1, n_G1, Ere, "mt0"), (cG1, n_G1, nEim, "mt6"), (cG2, n_G2, nEim, "mt6"), (cG2, n_G2, nEre, "mt5")]:
                                    pb_, pbn = next_mb()
                                    tt("pool", pb_, Gx, Ex, ALU.mult, [Gn, En], [pbn])
                                    prods.append((pb_, pbn))
                                for j, (pb_, pbn) in enumerate(prods):
                                    mm(pbank[t][hs, :], CT[:, j % 2, idx, :], pb_, False, True, ["wbuf0", pbn], ["pb%d" % t], sgc=True)
                            return stage_b
                        if pending[0] is not None:
                            later.append(pending[0])
                        pending[0] = make_stage_b()
                        c0 = 255 if d == 0 else 0
                        for jj in range(2):
                            cc_ = c0 + 256 * jj
                            col = lambda a: a[:, cc_:cc_ + 1]
                            act(sm[:, 48 + jj:49 + jj], col(cG1), AF.Copy, [n_G1, "mt0"], ["smx%d" % jj], scale=col(Ere))
                            act(s5ov[:, 0, idx, 2 * t + jj:2 * t + jj + 1], col(cG2), AF.Identity, [n_G2, "mt6", "smx%d" % jj], ["s5o"],
                                scale=col(nEim), bias=sm[:, 48 + jj:49 + jj])
                            act(sm[:, 52 + jj:53 + jj], col(cG1), AF.Copy, [n_G1, "mt1"], ["smz%d" % jj], scale=col(Eim))
                            act(s5ov[:, 1, idx, 2 * t + jj:2 * t + jj + 1], col(cG2), AF.Identity, [n_G2, "mt0", "smz%d" % jj], ["s5o"],
                                scale=col(Ere), bias=sm[:, 52 + jj:53 + jj])
                        cj = 2 * t + (1 if d == 0 else 0)
                        act(s5c[:, 0:1], s5ov[:, 0, idx, cj:cj + 1], AF.Copy, ["s5o", "consts"], ["s5c"], scale=mcar)
                        act(s5c[:, 1:2], s5ov[:, 1, idx, cj:cj + 1], AF.Copy, ["s5o", "consts"], ["s5c"], scale=mcar)
                        while later:
                            later.pop(0)()
                    if pending[0] is not None:
                        pending[0]()
                        pending[0] = None
            for t in range(NT):
                dma(next_q(), g1, zscr[2304 + fc * 128:2304 + (fc + 1) * 128, tsl(t)], ["zscr9_%d" % fc], ["mt7"])
                stt("dve", G1, g1, s5vec[:, l * 4 + fc:l * 4 + fc + 1], pbank[t][:], ALU.mult, ALU.add, ["mt7", "s5vec", "pb%d" % t], ["mt8"])
                tt("pool", g2, G1, G1, ALU.mult, ["mt8"], ["mt9"])
                ts("pool", g2, g2, 0.044715, 1.0, ALU.mult, ALU.add, ["mt9"], ["mt9"])
                tt("pool", g2, g2, G1, ALU.mult, ["mt9", "mt8"], ["mt9"])
                act(G2, g2, AF.Sigmoid, ["mt9"], ["mt10"], scale=1.5957691216)
                tt("pool", ygb[:, fc, tsl(t)], G1, G2, ALU.mult, ["mt8", "mt10"], ["stg1"])
        dma("sp", s5st_d[:, l * 256:(l + 1) * 256], s5o[:], ["s5o"], ["s5st_d"])
        for oc in range(2):
            for t in range(NT):
                ps, psn = next_ps()
                for kc in range(2):
                    mm(ps[:], gluW[:, kc, oc * 128:(oc + 1) * 128], ygb[:, kc, tsl(t)], kc == 0, kc == 1, ["gluW", "stg1"], [psn])
                sg, sgn = next_mt()
                act(sg, ps[:], AF.Sigmoid, [psn, "s5vec"], [sgn], bias=s5vec[:, l * 4 + 2 + oc:l * 4 + 3 + oc])
                tt("pool", mixed[:, 6 + oc, tsl(t)], ygb[:, oc, tsl(t)], sg, ALU.mult, ["stg1", sgn], [mn(6 + oc, t)])


    def mixer_hgrn(l):
        pbb = pbank[4].bitcast(BF16)
        A = [mt[i] for i in range(8)]
        An = ["mt%d" % i for i in range(8)]
        Bf = [mb[i] for i in range(6)] + [hrawb[:, 8192 + i * 512:8192 + (i + 1) * 512] for i in range(8)]
        Bn = ["mb%d" % i for i in range(6)] + ["mt%d" % (8 + i // 2) for i in range(8)]
        scm = hrawb[:, 12288 + 2048:12288 + 3072]
        qi, qo, qd, kd, kdec, vb = Bf[0], Bf[1], Bf[2], Bf[3], Bf[6], Bf[7]
        qin, qon, qdn, kdn, kdecn, vbn = Bn[0], Bn[1], Bn[2], Bn[3], Bn[6], Bn[7]
        ktok, ktokn = Bf[8], Bn[8]
        ko = [Bf[9], Bf[10], Bf[11]]
        kon = [Bn[9], Bn[10], Bn[11]]
        e16 = hgs[:, 0:33]
        dec = hgs[:, 36:44]
        so_i = [0]
        tile_ctr = [0]
        for d in range(2):
            for hp in range(2):
                a_ = d * 2 + hp
                lb_ap = hgl[:, 32 + a_ * 4 + l:32 + a_ * 4 + l + 1]
                oml_ap = hgl[:, 48 + a_ * 4 + l:48 + a_ * 4 + l + 1]
                dma("sp", hgS[:], hginit_d[:, ((l * 2 + d) * 2 + hp) * 64:((l * 2 + d) * 2 + hp) * 64 + 64], [], ["hgS"])
                for i in range(3):
                    memset("pool", ko[i], 0.0, [kon[i]])
                memset("dve", hgs[:, 0:33], 0.0, ["e16"])
                torder = list(range(NT)) if d == 0 else list(range(NT - 1, -1, -1))
                zfrow = (1792 if d == 0 else 2048) + hp * 128
                for t in torder:
                    lf, b_, ea, kk, m16 = A[1], A[2], A[3], A[4], A[6]
                    par_ = tile_ctr[0] % 2
                    tile_ctr[0] += 1
                    if par_ == 0:
                        zf, zfn, qs, qsn = tmp[0][:, :], "tmp0", tmp[1][:, :], "tmp1"
                        vb, vbn = sqb[0][:, :], "sqb0"
                    else:
                        zf, zfn, qs, qsn = tmp[2][:, :], "tmp2", tmp[3][:, :], "tmp3"
                        vb, vbn = sqb[1][:, :], "sqb1"
                    dma(next_q(), zf, zscr[zfrow:zfrow + 128, tsl(t)], ["zscr%d_%d" % (7 + d, hp)], [zfn])
                    dma(next_q(), qs, zscr[1024 + hp * 128:1024 + hp * 128 + 128, tsl(t)], ["zscr4_%d" % hp], [qsn])
                    dma("pool", vb.rearrange("p (s n) -> p s n", s=4),
                        ztok[1][t * 512:(t + 1) * 512, hp * 128:(hp + 1) * 128].rearrange("(s p) n -> p s n", p=128), ["ztok1"], [vbn])
                    act(zf, zf, AF.Sigmoid, [zfn], [zfn])
                    act(qs, qs, AF.Silu, [qsn], [qsn])
                    ts("dve", zf, zf, oml_ap, lb_ap, ALU.mult, ALU.add, [zfn, "hgl"], [zfn])
                    ts("dve", kk, zf, -1.0, 1.0, ALU.mult, ALU.add, [zfn], [An[4]])
                    ts("pool", lf, zf, 3.0e38, 1e-30, ALU.min, ALU.max, [zfn], [An[1]])
                    act(lf, lf, AF.Ln, [An[1]], [An[1]])
                    if d == 0:
                        p.op("dve", lambda h: h.tensor_tensor_scan(out=b_, data0=hgcm[:, :], data1=lf, initial=0.0, op0=ALU.mult, op1=ALU.add),
                             [An[1], "hgcm"], [An[2]])
                    else:
                        p.op("dve", lambda h: h.tensor_tensor_scan(out=rev(b_, 512), data0=hgcm[:, :], data1=rev(lf, 512), initial=0.0,
                                                                   op0=ALU.mult, op1=ALU.add), [An[1], "hgcm"], [An[2]])
                    if d == 0:
                        copy("dve", hgs[:, 1:33], cols(b_, 15, 16, 32), [An[2]], ["e16"])
                        rb = bass.AP(hgs, 0, [[48, 128], [1, 32], [0, 16]])
                        bend_b = bass.AP(hgs, 4, [[48, 128], [4, 8], [0, 64]])
                        act(dec, cols(hgs[:, 0:33], 4, 4, 8), AF.Exp, ["e16"], ["hgs_d"])
                    else:
                        copy("dve", hgs[:, 0:32], cols(b_, 0, 16, 32), [An[2]], ["e16"])
                        rb = bass.AP(hgs, 1, [[48, 128], [1, 32], [0, 16]])
                        bend_b = bass.AP(hgs, 0, [[48, 128], [4, 8], [0, 64]])
                        act(dec, cols(hgs[:, 0:33], 0, 4, 8), AF.Exp, ["e16"], ["hgs_d"])
                    copy("dve", m16[:, 0:32], cols(b_, 8, 16, 32), [An[2]], [An[6]])
                    mbc = bass.AP(m16.tensor, m16.offset, [[m16.ap[0][0], 128], [1, 32], [0, 16]])
                    v32 = lambda ap: ap.rearrange("p (n k) -> p n k", n=32)
                    v8 = lambda ap: ap.rearrange("p (j k) -> p j k", j=8)
                    A8, A8n = mt[11], "mt11"
                    act(A[0], b_, AF.Exp, [An[2]], [An[0]])
                    tt("pool", qi, qs, A[0], ALU.mult, [qsn, An[0]], [qin])
                    tt("dve", v8(A[1]), bend_b, v8(b_), ALU.subtract, ["e16", An[2]], [An[1]])
                    act(A[1], A[1], AF.Exp, [An[1]], [An[1]])
                    tt("pool", kdec, kk, A[1], ALU.mult, [An[4], An[1]], [kdecn])
                    tt("dve", v32(ea), v32(b_), rb, ALU.subtract, ["e16", An[2]], [An[3]])
                    ts("dve", ea, ea, 0.0, None, ALU.min, ALU.bypass, [An[3]], [An[3]])
                    act(ea, ea, AF.Exp, [An[3]], [An[3]])
                    tt("pool", qo, qs, ea, ALU.mult, [qsn, An[3]], [qon])
                    tt("dve", v32(A[7]), v32(b_), mbc, ALU.subtract, [An[6], An[2]], [An[7]])
                    act(A[6], A[7], AF.Exp, [An[7]], [An[6]])
                    tt("dve", qd, qs, A[6], ALU.mult, [qsn, An[6]], [qdn])
                    act(A8, A[7], AF.Exp, [An[7]], [A8n], scale=-1.0)
                    tt("dve", kd, kk, A8, ALU.mult, [An[4], A8n], [kdn])
                    for i in range(3):
                        if d == 0:
                            w_ = 16 * (i + 1)
                            Rb = bass.AP(hgs, i + 1, [[48, 128], [4, 8], [0, w_]])
                            sl_ = slice(0, w_)
                        else:
                            w_ = 64 - 16 * (i + 1)
                            Rb = bass.AP(hgs, i + 1, [[48, 128], [4, 8], [0, w_]])
                            sl_ = slice(16 * (i + 1), 64)
                        Tk, Tkn = [(A[7], An[7]), (A[0], An[0]), (A[1], An[1])][i]
                        tt("dve", v8(Tk)[:, :, sl_], Rb, v8(b_)[:, :, sl_], ALU.subtract, ["e16", An[2]], [Tkn])
                        act(v8(Tk)[:, :, sl_], v8(Tk)[:, :, sl_], AF.Exp, [Tkn], [Tkn])
                        tt("pool", v8(ko[i])[:, :, sl_], v8(kk)[:, :, sl_], v8(Tk)[:, :, sl_], ALU.mult, [An[4], Tkn], [kon[i]])
                    for sub in range(4):
                        p.op("pe", lambda h, sub=sub: h.transpose(out=pbb[:, sub * 128:(sub + 1) * 128], in_=kdec[:, sub * 128:(sub + 1) * 128],
                                                                   identity=identb[:]), [kdecn, "identb"], ["pb4"])
                    act(ktok, pbb[:, 0:512], AF.Copy, ["pb4"], [ktokn])
                    for bk in (0, 1, 6, 7):
                        memset("dve", pbank[bk][:], 0.0, ["pb%d" % bk])
                    for j in range(8):
                        par = j % 2
                        cs = slice(64 * j, 64 * j + 64)
                        for h2 in range(2):
                            ks = slice(64 * h2, 64 * h2 + 64)
                            mm(pbank[h2][64 * par:64 * par + 64, j * 64:j * 64 + 64], kd[ks, cs], qd[ks, cs], True, True, [kdn, qdn], ["pb%d" % h2])
                            for i in range(3):
                                I = i + 1 if d == 0 else i
                                tcs = slice(64 * j + 16 * I, 64 * j + 16 * I + 16)
                                mm(pbank[6 + h2][64 * par:64 * par + 64, j * 64 + 16 * I:j * 64 + 16 * I + 16], ko[i][ks, cs], qo[ks, tcs], True, True,
                                   [kon[i], qon], ["pb%d" % (6 + h2)])
                    mk = bass.AP(hgmask, 64 * d, [[128, 128], [0, 8], [1, 64]])
                    for h2 in range(2):
                        tt("dve", v8(A[7]), v8(pbank[h2][:]), mk, ALU.mult, ["pb%d" % h2, "hgmask"], [An[7]])
                        tt("dve", scm[:, h2 * 512:(h2 + 1) * 512], A[7], pbank[6 + h2][:], ALU.add, [An[7], "pb%d" % (6 + h2)], ["mb4", "mb5"])
                    for j in range(8):
                        par, sub = j % 2, j // 2
                        ts_ = slice(64 * par, 64 * par + 64)
                        ub_ = 3 if par == 0 else 5
                        for h2 in range(2):
                            mm(pbank[ub_][64 * h2:64 * h2 + 64, j * 64:(j + 1) * 64], ktok[ts_, sub * 128 + 64 * h2:sub * 128 + 64 * h2 + 64],
                               vb[ts_, sub * 128 + 64 * h2:sub * 128 + 64 * h2 + 64], True, True, [ktokn, vbn], ["pb%d" % ub_])
                    memset("dve", pbank[2][:], 0.0, ["pb2"])
                    jorder = list(range(8)) if d == 0 else list(range(7, -1, -1))
                    for j in jorder:
                        par, sub = j % 2, j // 2
                        cs = slice(64 * j, 64 * j + 64)
                        J = 8 * t + j
                        first = (J == 0) if d == 0 else (J == 31)
                        seq_start = (J % 4 == 0) if d == 0 else (J % 4 == 3)
                        seq_end = (J % 4 == 3) if d == 0 else (J % 4 == 0)
                        if seq_start and not first:
                            ts("dve", hgS[:], hgS[:], mcar, None, ALU.mult, ALU.bypass, ["hgS", "consts"], ["hgS"])
                        for h2 in range(2):
                            ks = slice(64 * h2, 64 * h2 + 64)
                            copy("dve", hgSb[ks, 64 * h2:64 * h2 + 64], hgS[ks, :], ["hgS"], ["hgSb"])
                        for h2 in range(2):
                            ks = slice(64 * h2, 64 * h2 + 64)
                            mm(pbank[2][ks, cs], vb[:, sub * 128 + 64 * h2:sub * 128 + 64 * h2 + 64], scm[:, (h2 * 8 + j) * 64:(h2 * 8 + j) * 64 + 64],
                               False, True, [vbn, "mb4", "mb5"], ["pb2"], sgc=True)
                        mm(pbank[2][:, cs], hgSb[:, :], qi[:, cs], False, True, ["hgSb", qin], ["pb2"], sgc=True)
                        ub_ = 3 if par == 0 else 5
                        stt("dve", hgS[:], hgS[:], dec[:, j:j + 1], pbank[ub_][:, j * 64:(j + 1) * 64], ALU.mult, ALU.add,
                            ["pb%d" % ub_, "hgs_d", "hgS"], ["hgS"])
                        if seq_end:
                            so, son = hgSo[so_i[0] % 2], "hgSo%d" % (so_i[0] % 2)
                            so_i[0] += 1
                            copy("dve", so[:], hgS[:], ["hgS"], [son])
                            off = ((((l * 2 + d) * 2 + hp) * 8) + J // 4) * 64
                            dma("sp", hgst_d[:, off:off + 64], so[:], [son], ["hgst_d"])
                    o_, gz, t1 = A[5], A[1], A[3]
                    if d == 0:
                        copy("dve", o_, pbank[2][:], ["pb2"], [An[5]])
                        dma("sp", oscr[hp * 128:(hp + 1) * 128, tsl(t)], o_, [An[5]], ["oscr%d" % hp])
                    else:
                        dma(next_q(), o_, oscr[hp * 128:(hp + 1) * 128, tsl(t)], ["oscr%d" % hp], [An[5]])
                        tt("dve", o_, o_, pbank[2][:], ALU.add, [An[5], "pb2"], [An[5]])
                        act(Bf[2], o_, AF.Square, [An[5]], [Bn[2]])
                        mm(pbank[4][:], hgblk[:], Bf[2], True, True, ["hgblk", Bn[2]], ["pb4"])
                        act(t1, pbank[4][:], AF.Sqrt, ["pb4"], [An[3]], bias=EPS)
                        p.op("dve", lambda h: h.reciprocal(out=t1, in_=t1), [An[3]], [An[3]])
                        dma(next_q(), gz, zscr[1536 + hp * 128:1536 + hp * 128 + 128, tsl(t)], ["zscr6_%d" % hp], [An[1]])
                        act(gz, gz, AF.Silu, [An[1]], [An[1]])
                        tt("dve", o_, o_, t1, ALU.mult, [An[5], An[3]], [An[5]])
                        stt("dve", mixed[:, 4 + hp, tsl(t)], o_, hgng[:, l * 2 + hp:l * 2 + hp + 1], gz, ALU.mult, ALU.mult,
                            [An[5], "hgng", An[1]], [mn(4 + hp, t)])

    for l in range(nlayers):
        b0 = l * 48
        cur_l[0] = l
        norm_mod(lambda k: gm1[:, l * 8 + k:l * 8 + k + 1], lambda k: adasb[:, b0 + k:b0 + k + 1], "gm1_%d" % l,
                 lambda k, t: (hb[:, k, tsl(t)], hn(k, t)))
        for blk in range(10):
            w_, wn = next_w()
            w_ = w_[:, 0:2048].rearrange("p (k n) -> p k n", k=8)
            dma("pool", w_, w_in[l][:, blk * 256:(blk + 1) * 256].rearrange("(k p) n -> p k n", p=128), [], [wn])
            if blk in (3, 5):
                zi = 0 if blk == 3 else 1
                for half in range(2):
                    s_, sn = next_stg()
                    for j in range(8):
                        t16 = half * 8 + j
                        ps, psn = next_ps()
                        for k in range(8):
                            mm(ps[:, 0:256], hb[:, k, t16 * 128:(t16 + 1) * 128], w_[:, k, 0:256], k == 0, k == 7,
                               [wn, hn(k, t16 // 4)], [psn])
                        act(s_[:, j * 256:(j + 1) * 256], ps[:, 0:256], AF.Copy, [psn], [sn])
                    dma(next_q(), ztok[zi][half * 1024:(half + 1) * 1024, :].rearrange("(j p) n -> p j n", p=128),
                        s_[:].rearrange("p (j n) -> p j n", n=256), [sn], ["ztok%d" % zi])
                continue
            for c in range(2):
                s_, sn = next_stg()
                for t in range(NT):
                    ps, psn = next_ps()
                    for k in range(8):
                        mm(ps[:], w_[:, k, c * 128:(c + 1) * 128], hb[:, k, tsl(t)], k == 0, k == 7, [wn, hn(k, t)], [psn])
                    act(s_[:, tsl(t)], ps[:], AF.Copy, [psn], [sn])
                dma(next_q(), zscr[blk * 256 + c * 128: blk * 256 + (c + 1) * 128, :], s_[:], [sn], ["zscr%d_%d" % (blk, c)])

        p.barrier()
        zero_chunks = []
        if not mix["conv"]:
            zero_chunks += [0, 1]
        if not mix["gmlp"]:
            zero_chunks += [2, 3]
        if not mix["hgrn"]:
            zero_chunks += [4, 5]
        if not mix["s5"]:
            zero_chunks += [6, 7]
        for k in zero_chunks:
            memset("pool", mixed[:, k, :], 0.0, [mn(k, t) for t in range(NT)])
        if mix["gmlp"]:
            mixer_gmlp(l)
        if mix["conv"]:
            mixer_conv(l)
        if mix["s5"]:
            p.barrier()
            mixer_s5(l)
        if mix["hgrn"]:
            mixer_hgrn(l)

        for oc in range(8):
            if oc % 4 == 0:
                w_, wn = next_w()
                w_ = w_[:, :].rearrange("p (k n) -> p k n", k=8)
                dma("pool", w_, w_out[l][:, (oc // 4) * 512:(oc // 4 + 1) * 512].rearrange("(k p) n -> p k n", p=128), [], [wn])
            for t in range(NT):
                ps, psn = next_ps()
                for k in range(8):
                    mm(ps[:], w_[:, k, (oc % 4) * 128:(oc % 4 + 1) * 128], mixed[:, k, tsl(t)], k == 0, k == 7, [wn, mn(k, t)], [psn])
                stt("dve", x[:, oc, tsl(t)], ps[:], adasb[:, b0 + 16 + oc:b0 + 17 + oc], x[:, oc, tsl(t)], ALU.mult, ALU.add,
                    [psn, "ada%d" % l, xn(oc, t)], [xn(oc, t)])
        p.barrier()
        norm_mod(lambda k: gm2[:, l * 8 + k:l * 8 + k + 1], lambda k: adasb[:, b0 + 24 + k:b0 + 25 + k], "gm2_%d" % l,
                 lambda k, t: (hb[:, k, tsl(t)], hn(k, t)))
        for hbk in range(8):
            w1, w1n = next_w()
            w1 = w1[:, :].rearrange("p (k n) -> p k n", k=8)
            dma("pool", w1, mlp_w1[l][:, hbk * 512:(hbk + 1) * 512].rearrange("(k p) n -> p k n", p=128), [], [w1n])
            w2, w2n = next_w()
            w2 = w2[:, :].rearrange("p (c n) -> p c n", c=4)
            dma("pool", w2, mlp_w2[l][hbk * 512:(hbk + 1) * 512, :].rearrange("(c p) n -> p c n", p=128), [], [w2n])
            for hc in range(4):
                for t in range(NT):
                    ps, psn = next_ps()
                    for k in range(8):
                        mm(ps[:], w1[:, k, hc * 128:(hc + 1) * 128], hb[:, k, tsl(t)], k == 0, k == 7, [w1n, hn(k, t)], [psn])
                    tm, tmn = next_tmp()
                    act(tm[:], ps[:], AF.Relu, [psn], [tmn])
                    tt("pool", mixed[:, hc, tsl(t)], tm[:], tm[:], ALU.mult, [tmn], [mn(hc, t)])
            for oc in range(8):
                for t in range(NT):
                    ps, psn = next_ps()
                    for hc in range(4):
                        mm(ps[:], w2[:, hc, oc * 128:(oc + 1) * 128], mixed[:, hc, tsl(t)], hc == 0, hc == 3, [w2n, mn(hc, t)], [psn])
                    stt("dve", x[:, oc, tsl(t)], ps[:], adasb[:, b0 + 40 + oc:b0 + 41 + oc], x[:, oc, tsl(t)], ALU.mult, ALU.add,
                        [psn, "ada%d" % l, xn(oc, t)], [xn(oc, t)])

    fin = []

    def fin_out(k, t):
        return stg[k % 2][:, tsl(t)], "stg%d" % (k % 2)

    for t in range(NT):
        ps, psn = pbank[4 + (t % 2)], "pb%d" % (4 + (t % 2))
        for k in range(8):
            sq, sqn = next_sq()
            act(sq[:], x[:, k, tsl(t)], AF.Square, [xn(k, t)], [sqn])
            mm(ps[:], ones_bf[:], sq[:], k == 0, k == 7, [sqn, "ones"], [psn])
        sd, sdn = next_tmp()
        act(sd[:], ps[:], AF.Sqrt, [psn], [sdn], bias=EPS, scale=1.0 / D)
        rs, rsn = next_tmp()
        p.op("dve", lambda h, rs=rs, sd=sd: h.reciprocal(out=rs[:], in_=sd[:]), [sdn], [rsn])
        for k in range(8):
            tm, tmn = next_tmp()
            if tmn == rsn:
                tm, tmn = next_tmp()
            stt("dve", tm[:], x[:, k, tsl(t)], fng[:, k:k + 1], rs[:], ALU.mult, ALU.mult, [xn(k, t), rsn, "fng"], [tmn])
            fin.append(dma(next_q(), yout[k * 128:(k + 1) * 128, tsl(t)], tm[:], [tmn], []))

    o = p.op("sp", lambda h: h.nop())
    o.deps = fin
    for f in fin:
        f.sig = True
    p.emit()
    st.close()
    return nc


_NC_CACHE = {}


def kernel(**inp):
    f = lambda a: np.ascontiguousarray(np.asarray(a, dtype=np.float32))
    x_prompt = f(inp["x_prompt"])
    x_sample = f(inp["x_sample"])
    key = "main"
    if key not in _NC_CACHE:
        _NC_CACHE[key] = build_program()
    nc = _NC_CACHE[key]
    shared = {k: f(inp[k]) for k in ["ada_w", "w_in", "w_out", "mlp_w1", "mlp_w2"]}
    fm = lambda a, n: np.ascontiguousarray(f(a).reshape(-1, n, 128).transpose(2, 0, 1).reshape(128, -1))
    shared["norm1_gT"] = fm(inp["norm1_g"], 8)
    shared["norm2_gT"] = fm(inp["norm2_g"], 8)
    shared["ada_bT"] = fm(inp["ada_b"], 48)
    shared["final_norm_gT"] = fm(inp["final_norm_g"], 8)
    shared["gmlp_norm_g"] = f(inp["gmlp_norm_g"])
    shared["gmlp_wsT"] = np.ascontiguousarray(f(inp["gmlp_ws"]).transpose(0, 1, 3, 2))
    shared["gmlp_bs"] = f(inp["gmlp_bs"])
    cw = f(inp["conv_w"])
    shared["conv_wT"] = np.ascontiguousarray(cw.reshape(L, 31, 2, 128).transpose(3, 0, 2, 1).reshape(128, L * 62))
    cvv = np.stack([f(inp["conv_b"]), f(inp["conv_ln_g"]), f(inp["conv_ln_b"])], axis=1)
    shared["conv_vecT"] = np.ascontiguousarray(cvv.reshape(L, 3, 2, 128).transpose(3, 0, 1, 2).reshape(128, L * 6))
    shared["ident"] = np.eye(128, dtype=np.float32)
    def st_lay(a):
        a = f(a)
        lead = a.shape[:-2]
        return a.reshape(lead + (8, 128)).transpose((len(lead) + 1,) + tuple(range(len(lead))) + (len(lead),))
    ldt_e = np.repeat(f(inp["s5_log_dt"])[..., None], 64, axis=-1)
    s5p = np.stack([st_lay(inp["s5_a_re"]), st_lay(inp["s5_a_im"]), st_lay(ldt_e)], axis=2)
    shared["s5pT"] = np.ascontiguousarray(s5p.reshape(128, L * 48))
    BTb = np.zeros((L, 128, 2, 2, 2, 2, 128), np.float32)
    CTb = np.zeros((L, 128, 2, 16, 64), np.float32)
    for ri, (bb, cc) in enumerate([(f(inp["s5_b_re"]), f(inp["s5_c_re"])), (f(inp["s5_b_im"]), f(inp["s5_c_im"]))]):
        for d_ in range(2):
            for g in range(16):
                q = (g // 2) % 4
                BTb[:, (g % 8) * 16:(g % 8) * 16 + 16, ri, d_, g // 8, q % 2, (g % 2) * 64:(g % 2) * 64 + 64] = bb[:, d_, g].transpose(0, 2, 1)
                CTb[:, (g % 2) * 64:(g % 2) * 64 + 64, ri, d_ * 8 + g // 2, 32 * (q % 2) + 16 * (g % 2):32 * (q % 2) + 16 * (g % 2) + 16] = cc[:, d_, g].transpose(0, 2, 1)
    shared["s5_BTblk"] = np.ascontiguousarray(BTb.reshape(L, 128, 2048))
    shared["s5_CTblk"] = np.ascontiguousarray(CTb.reshape(L, 128, 2048))
    sv = np.stack([f(inp["s5_d"]), f(inp["s5_glu_b"])], axis=1)
    shared["s5vecT"] = np.ascontiguousarray(sv.reshape(L, 2, 2, 128).transpose(3, 0, 1, 2).reshape(128, L * 4))
    shared["s5_glu_w"] = f(inp["s5_glu_w"])
    s5i_zero = np.zeros((128, L * 32), np.float32)
    lbl = f(inp["hgrn_lb_logits"])
    shared["hglbT"] = np.ascontiguousarray(lbl.reshape(2, L, 2, 128).transpose(3, 0, 2, 1).reshape(128, 16))
    shared["hgngT"] = np.ascontiguousarray(f(inp["hgrn_norm_g"]).reshape(L, 2, 128).transpose(2, 0, 1).reshape(128, L * 2))
    ii = np.arange(64)
    same = (ii[:, None] // 16) == (ii[None, :] // 16)
    mf = ((ii[:, None] <= ii[None, :]) & same).astype(np.float32)
    mbk = ((ii[:, None] >= ii[None, :]) & same).astype(np.float32)
    shared["hgmask"] = np.ascontiguousarray(np.concatenate([np.concatenate([mf, mbk], axis=1)] * 2, axis=0))
    cmk = np.ones((128, 512), np.float32)
    cmk[:, ::64] = 0.0
    shared["hgcmask"] = cmk
    blk = np.zeros((128, 128), np.float32)
    blk[:64, :64] = 1.0 / 64
    blk[64:, 64:] = 1.0 / 64
    shared["hgblk"] = blk
    hgi_zero = np.zeros((128, L * 256), np.float32)
    def hg_lay(a):
        a = f(a).reshape(L, 2, 2, 2, 64, 64)
        return np.ascontiguousarray(a.transpose(3, 4, 0, 1, 2, 5).reshape(128, L * 256))
    in_maps = []
    for c in range(8):
        m = dict(shared)
        if c < 4:
            m["xin"] = np.ascontiguousarray(x_prompt[8 * c:8 * c + 8].reshape(T, D).T)
            m["cond"] = fm(inp["c_ctx"], 8)
            m["mcar"] = np.zeros((128, 1), np.float32)
            m["s5init"] = s5i_zero
            m["hginit"] = hgi_zero
        else:
            m["xin"] = np.ascontiguousarray(x_sample[c - 4].T)
            m["cond"] = fm(inp["c"][c - 4], 8)
            m["mcar"] = np.ones((128, 1), np.float32)
            b = c - 4
            si = np.stack([st_lay(inp["state_s5_re"][b]), st_lay(inp["state_s5_im"][b])], axis=2)
            m["s5init"] = np.ascontiguousarray(si.reshape(128, L * 32))
            m["hginit"] = hg_lay(inp["state_hgrn"][b])
        in_maps.append(m)
    res = run_bass_kernel_spmd(nc, in_maps, core_ids=list(range(8)))
    r = res.results
    y_prompt = np.stack([r[c]["yout"].T.reshape(8, 256, D) for c in range(4)]).reshape(32, 256, D)
    y_sample = np.stack([r[c]["yout"].T for c in range(4, 8)])
    def hg_unlay(c):
        a = r[c]["hgst"].reshape(2, 64, L, 2, 2, 8, 64)
        return a.transpose(5, 2, 3, 4, 0, 1, 6).reshape(8, L, 2, 4, 64, 64)
    hg = np.ascontiguousarray(np.concatenate([hg_unlay(c) for c in range(4)], axis=0))
    def s5_unlay(c):
        a = r[c]["s5st"].reshape(2, 64, L, 2, 2, 8, 8)
        return a.transpose(3, 6, 2, 4, 5, 0, 1).reshape(2, 8, L, 2, 16, 64)
    s5all = np.concatenate([s5_unlay(c) for c in range(4)], axis=1)
    s5r = np.ascontiguousarray(s5all[0])
    s5i = np.ascontiguousarray(s5all[1])
    return (np.ascontiguousarray(y_prompt), np.ascontiguousarray(y_sample), hg, s5r, s5i)
```
